# Optimizing a Trainium2 kernel written in Bass

```python
import math
import jax, jax.numpy as jnp
from jax import lax
import numpy as np

D_MODEL = 4096
BATCH = 1
SEQ = 8192
DEPTH = 1

N_HEADS = 16
HEAD_DIM = 128
V_DIM = 2 * HEAD_DIM
ATTN_WIDTH = N_HEADS * V_DIM
CONV_CH = D_MODEL
CONV_WIDTH = 31
CONV_PAD = CONV_WIDTH // 2
D_FF = 4 * D_MODEL
PLE_DIM = 256
REL_BUCKETS = 32
REL_MAX_DIST = 128
Q_BLOCK = 128
EPS = 1e-6

Q_COLS = 2 * N_HEADS * HEAD_DIM
K_COLS = 2 * N_HEADS * HEAD_DIM
V_COLS = N_HEADS * V_DIM
CONV_COLS = 2 * CONV_CH
GATE_COLS = 2 * D_MODEL
IN_COLS = Q_COLS + K_COLS + V_COLS + CONV_COLS + GATE_COLS
SPLITS = (Q_COLS, Q_COLS + K_COLS, Q_COLS + K_COLS + V_COLS,
          Q_COLS + K_COLS + V_COLS + CONV_COLS)

kernel_name = "hybrid_diffattn_conformer_gated_encoder_layer"


def rms_norm(x, g):
    xf = x.astype(jnp.float32)
    y = xf * lax.rsqrt(jnp.mean(xf * xf, axis=-1, keepdims=True) + EPS)
    return (y * g.astype(jnp.float32)).astype(x.dtype)


def layer_norm(x, g, b):
    xf = x.astype(jnp.float32)
    mu = jnp.mean(xf, axis=-1, keepdims=True)
    var = jnp.mean(jnp.square(xf - mu), axis=-1, keepdims=True)
    y = (xf - mu) * lax.rsqrt(var + EPS)
    return (y * g.astype(jnp.float32) + b.astype(jnp.float32)).astype(x.dtype)


def rel_bucket(rel):
    nb = REL_BUCKETS // 2
    max_exact = nb // 2
    ret = (rel > 0).astype(jnp.int32) * nb
    n = jnp.abs(rel)
    nf = jnp.maximum(n, 1).astype(jnp.float32)
    large = max_exact + (jnp.log(nf / max_exact) / math.log(REL_MAX_DIST / max_exact)
                         * (nb - max_exact)).astype(jnp.int32)
    large = jnp.minimum(large, nb - 1)
    return ret + jnp.where(n < max_exact, n, large)


def diff_attention(q1, q2, k1, k2, v, positions, rel_table, lam):
    B, H, S, _ = q1.shape
    nblk = S // Q_BLOCK
    scale = HEAD_DIM ** -0.5

    def to_blocks(t):
        return t.reshape(B, H, nblk, Q_BLOCK, t.shape[-1]).transpose(2, 0, 1, 3, 4)

    qb1, qb2 = to_blocks(q1), to_blocks(q2)
    pb = positions.reshape(B, nblk, Q_BLOCK).transpose(1, 0, 2)

    def one_block(args):
        bq1, bq2, bp = args
        rel = positions[:, None, :] - bp[:, :, None]
        bias = jnp.moveaxis(rel_table[rel_bucket(rel)], -1, 1).astype(jnp.float32)
        s1 = jnp.einsum('bhqd,bhkd->bhqk', bq1, k1).astype(jnp.float32) * scale + bias
        s2 = jnp.einsum('bhqd,bhkd->bhqk', bq2, k2).astype(jnp.float32) * scale + bias
        a = jax.nn.softmax(s1, axis=-1) - lam * jax.nn.softmax(s2, axis=-1)
        return jnp.einsum('bhqk,bhkd->bhqd', a.astype(v.dtype), v)

    out = lax.map(one_block, (qb1, qb2, pb))
    return out.transpose(1, 2, 0, 3, 4).reshape(B, H, S, V_DIM)


def setup_inputs(seed: int = 0) -> dict:
    key = jax.random.key(seed)
    ks = jax.random.split(key, 32)
    f32 = jnp.float32

    def nrm(k, shape, scale):
        return jax.random.normal(k, shape, f32) * scale

    def gain(k, shape):
        return 1.0 + 0.01 * jax.random.normal(k, shape, f32)

    L = DEPTH
    return {
        "x": jax.random.normal(ks[0], (BATCH, SEQ, D_MODEL), f32),
        "p": jax.random.normal(ks[1], (DEPTH, BATCH, SEQ, PLE_DIM), f32),
        "positions": jnp.broadcast_to(jnp.arange(SEQ, dtype=jnp.int32)[None, :], (BATCH, SEQ)),
        "rel_table": nrm(ks[2], (REL_BUCKETS, N_HEADS), 0.5),
        "mix_pre_g": gain(ks[3], (L, D_MODEL)),
        "w_in": nrm(ks[4], (L, D_MODEL, IN_COLS), D_MODEL ** -0.5),
        "lambda_q1": nrm(ks[5], (L, HEAD_DIM), 0.1),
        "lambda_k1": nrm(ks[6], (L, HEAD_DIM), 0.1),
        "lambda_q2": nrm(ks[7], (L, HEAD_DIM), 0.1),
        "lambda_k2": nrm(ks[8], (L, HEAD_DIM), 0.1),
        "subln_g": gain(ks[9], (L, V_DIM)),
        "w_attn_o": nrm(ks[10], (L, ATTN_WIDTH, D_MODEL), ATTN_WIDTH ** -0.5),
        "w_dw": nrm(ks[11], (L, CONV_WIDTH, 1, CONV_CH), CONV_WIDTH ** -0.5),
        "b_dw": nrm(ks[12], (L, CONV_CH), 0.01),
        "conv_ln_g": gain(ks[13], (L, CONV_CH)),
        "conv_ln_b": nrm(ks[14], (L, CONV_CH), 0.01),
        "w_conv_o": nrm(ks[15], (L, CONV_CH, D_MODEL), CONV_CH ** -0.5),
        "w_out": nrm(ks[16], (L, D_MODEL, D_MODEL), D_MODEL ** -0.5),
        "mix_post_g": gain(ks[17], (L, D_MODEL)),
        "ffn_pre_g": gain(ks[18], (L, D_MODEL)),
        "w_up": nrm(ks[19], (L, D_MODEL, D_FF), D_MODEL ** -0.5),
        "w_down": nrm(ks[20], (L, D_FF, D_MODEL), D_FF ** -0.5),
        "ffn_post_g": gain(ks[21], (L, D_MODEL)),
        "w_ple_gate": nrm(ks[22], (L, D_MODEL, D_MODEL), D_MODEL ** -0.5),
        "w_ple_proj": nrm(ks[23], (L, PLE_DIM, D_MODEL), PLE_DIM ** -0.5),
        "ple_post_g": gain(ks[24], (L, D_MODEL)),
    }


def reference(x, p, positions, rel_table, mix_pre_g, w_in, lambda_q1, lambda_k1,
              lambda_q2, lambda_k2, subln_g, w_attn_o, w_dw, b_dw, conv_ln_g,
              conv_ln_b, w_conv_o, w_out, mix_post_g, ffn_pre_g, w_up, w_down,
              ffn_post_g, w_ple_gate, w_ple_proj, ple_post_g):
    B, S, _ = x.shape
    for i in range(DEPTH):
        lambda_init = 0.8 - 0.6 * math.exp(-0.3 * i)

        h = rms_norm(x, mix_pre_g[i])
        cols = h @ w_in[i]
        q, k, v, conv_in, gate_in = jnp.split(cols, SPLITS, axis=-1)

        q = q.reshape(B, S, N_HEADS, 2, HEAD_DIM)
        k = k.reshape(B, S, N_HEADS, 2, HEAD_DIM)
        q1 = q[:, :, :, 0].transpose(0, 2, 1, 3)
        q2 = q[:, :, :, 1].transpose(0, 2, 1, 3)
        k1 = k[:, :, :, 0].transpose(0, 2, 1, 3)
        k2 = k[:, :, :, 1].transpose(0, 2, 1, 3)
        vh = v.reshape(B, S, N_HEADS, V_DIM).transpose(0, 2, 1, 3)
        lam = (jnp.exp(jnp.sum(lambda_q1[i].astype(jnp.float32) * lambda_k1[i].astype(jnp.float32)))
               - jnp.exp(jnp.sum(lambda_q2[i].astype(jnp.float32) * lambda_k2[i].astype(jnp.float32)))
               + lambda_init)
        o = diff_attention(q1, q2, k1, k2, vh, positions, rel_table, lam)
        o = rms_norm(o, subln_g[i]) * (1.0 - lambda_init)
        o = o.transpose(0, 2, 1, 3).reshape(B, S, ATTN_WIDTH)
        y_attn = o @ w_attn_o[i]

        c_val, c_gate = jnp.split(conv_in, 2, axis=-1)
        c = c_val * jax.nn.sigmoid(c_gate)
        c = lax.conv_general_dilated(
            c, w_dw[i], window_strides=(1,), padding=[(CONV_PAD, CONV_PAD)],
            dimension_numbers=('NWC', 'WIO', 'NWC'), feature_group_count=CONV_CH) + b_dw[i]
        c = jax.nn.silu(layer_norm(c, conv_ln_g[i], conv_ln_b[i]))
        y_conv = c @ w_conv_o[i]

        g_attn, g_conv = jnp.split(jax.nn.sigmoid(gate_in.astype(jnp.float32)).astype(x.dtype), 2, axis=-1)
        mix = (g_attn * y_attn + g_conv * y_conv) @ w_out[i]
        x = x + rms_norm(mix, mix_post_g[i])

        h = rms_norm(x, ffn_pre_g[i])
        u = jnp.square(jax.nn.relu(h @ w_up[i]))
        x = x + rms_norm(u @ w_down[i], ffn_post_g[i])

        ple_gate = jax.nn.sigmoid((x @ w_ple_gate[i]).astype(jnp.float32)).astype(x.dtype)
        e = p[i] @ w_ple_proj[i]
        x = x + rms_norm(ple_gate * e, ple_post_g[i])
    return x
```

```python
import math
import numpy as np
import ml_dtypes
import concourse.bass as bass
import concourse.mybir as mybir
from concourse.bass_utils import run_bass_kernel_spmd

F32 = mybir.dt.float32
BF16 = mybir.dt.bfloat16
U8 = mybir.dt.uint8
AF = mybir.ActivationFunctionType
ALU = mybir.AluOpType

NCORE = 8
SEQ = 8192
NT = 1024
HAL = 16
NTH = NT + 2 * HAL
D = 4096
DFF = 16384
NH = 16
EPS = 1e-6
LAMBDA_INIT = 0.8 - 0.6 * math.exp(0.0)
SCALE = 128 ** -0.5
ARENA = 211968
GW = 9216
GOFF = 8191
ZROW = GW + 1
DEBUG = False
NDBG = 24
DBG_NAMES = []

Q0, K0, V0, CV0, CG0, G0 = 0, 4096, 8192, 12288, 16384, 20480


def _dsize(dt):
    return 4 if dt == F32 else (2 if dt == BF16 else 1)


class Sem:
    def __init__(self, nc, name):
        self.h = nc.alloc_semaphore(name)
        self.v = 0


class Prog:
    ENG = ("pe", "act", "dve", "pool", "sp")

    def __init__(self, nc):
        self.nc = nc
        self.q = {k: [] for k in self.ENG}
        self.waited = {}
        self.all_sems = []
        self.n = 0
        self.prog = {k: self.sem("prog_" + k) for k in ("pe", "act", "dve", "pool")}
        self.dma_out = []

    def sem(self, name):
        self.n += 1
        sm = Sem(self.nc, "%s_%d" % (name, self.n))
        self.all_sems.append(sm)
        return sm

    def op(self, eng, fn, waits=(), incs=()):
        ws = []
        for w in waits:
            if w is None:
                continue
            s, v = w
            if v <= 0:
                continue
            key = (eng, id(s))
            if self.waited.get(key, 0) >= v:
                continue
            self.waited[key] = v
            ws.append((s.h, v))
        ii = []
        for s, a in incs:
            s.v += a
            ii.append((s.h, a))
        self.q[eng].append((fn, ws, ii))

    def c(self, eng, fn, waits=(), incs=(), signal=True):
        incs = list(incs)
        if signal:
            incs.append((self.prog[eng], 1))
        self.op(eng, fn, waits, incs)
        return (self.prog[eng], self.prog[eng].v)

    def d(self, eng, fn, sem, waits=()):
        self.op(eng, fn, waits, [(sem, 16)])
        ev = (sem, sem.v)
        self.dma_out.append(ev)
        return ev

    def barrier(self, engines=("pe", "act", "dve", "sp", "pool")):
        evs = [(self.prog[k], self.prog[k].v) for k in ("pe", "act", "dve")]
        seen = {}
        for s, v in self.dma_out:
            seen[id(s)] = (s, max(v, seen.get(id(s), (s, 0))[1]))
        evs += list(seen.values())
        self.dma_out = []
        for e in engines:
            self.op(e, None, evs, ())

    def emit(self, block):
        def mk(qn):
            def body(e):
                for fn, ws, ii in self.q[qn]:
                    for h, v in ws:
                        e.wait_ge(h, v)
                    if fn is None:
                        continue
                    ins = fn(e)
                    for h, a in ii:
                        ins = ins.then_inc(h, a)
            return body
        block.tensor(mk("pe"))
        block.scalar(mk("act"))
        block.vector(mk("dve"))
        block.gpsimd(mk("pool"))
        block.sync(mk("sp"))


def build():
    nc = bass.Bass("TRN2", target_bir_lowering=False)
    P = Prog(nc)

    def din(name, shape, dt=F32):
        return nc.dram_tensor(name, list(shape), dt, kind="ExternalInput").ap()

    def dtmp(name, shape, dt=F32):
        return nc.dram_tensor(name, list(shape), dt).ap()

    xT = din("xT", [D, NTH])
    xT_all = din("xT_all", [NCORE * D, NTH])
    pT = din("pT", [256, NT])
    vecs_d = din("vecs", [128, 8 * 32])
    wdw_d = din("wdw", [128, 32 * 31])
    lamv_d = din("lamv", [128, 4])
    subln_d = din("subln", [128, 2])
    relt_d = din("relt", [32, 16])
    oh_d = din("oh", [32, GW])
    ident_d = din("ident", [128, 128], BF16)
    w_in = din("w_in", [D, 28672])
    w_attn_o = din("w_attn_o", [D, D])
    w_conv_o = din("w_conv_o", [D, D])
    w_out = din("w_out", [D, D])
    w_up = din("w_up", [D, DFF])
    w_down = din("w_down", [DFF, D])
    w_ple_gate = din("w_ple_gate", [D, D])
    w_ple_proj = din("w_ple_proj", [256, D])
    outT = nc.dram_tensor("outT", [D, NT], F32, kind="ExternalOutput").ap()

    qT_d = dtmp("qT_d", [D, NT], BF16)
    kT_loc = dtmp("kT_loc", [D, NT], BF16)
    kT_all = dtmp("kT_all", [NCORE * D, NT], BF16)
    v_loc = dtmp("v_loc", [NT, D], BF16)
    v_all = dtmp("v_all", [SEQ, D], BF16)
    gate_d = dtmp("gate_d", [2 * D, NT])
    conv_d = dtmp("conv_d", [D, NT])
    stash_d = dtmp("stash_d", [D, NT])
    oT_d = dtmp("oT_d", [D, NT], BF16)
    mix_d = dtmp("mix_d", [D, NT])
    x1_d = dtmp("x1_d", [D, NT])
    ffn_d = dtmp("ffn_d", [D, NT])
    x2_d = dtmp("x2_d", [D, NT])
    t_d = dtmp("t_d", [D, NT])
    g_d = dtmp("g_d", [NH, GW])
    Z_t = nc.dram_tensor("Z_d", [NH * 128, ZROW], F32)

    dbg_names = []
    if DEBUG:
        dbg = nc.dram_tensor("dbg", [NDBG * 128, NT], F32, kind="ExternalOutput").ap()
        sem_dbg = P.sem("dbg")

    def dump(name, ap, n, eng="sp"):
        if not DEBUG:
            return
        i = len(dbg_names)
        assert i < NDBG
        dbg_names.append(name)
        P.barrier(engines=("pe", "act", "dve", "sp", "pool"))
        P.d(eng, lambda e: e.dma_start(out=dbg[i * 128:(i + 1) * 128, 0:n], in_=ap), sem_dbg)
        P.barrier(engines=("pe", "act", "dve", "sp", "pool"))

    arena = nc.alloc_sbuf_tensor("arena", [128, ARENA], U8)
    ps = nc.alloc_psum_tensor("ps", [128, 4096], F32)

    class M:
        top = 0

    def salloc(dt, *free):
        n = int(np.prod(free)) * _dsize(dt)
        off = M.top
        M.top += (n + 63) // 64 * 64
        assert M.top <= ARENA, ("SBUF overflow", M.top)
        ap = arena[:, off:off + n].bitcast(dt)
        if len(free) == 2:
            ap = ap.rearrange("p (a b) -> p a b", b=free[1])
        elif len(free) == 3:
            ap = ap.rearrange("p (a b c) -> p a b c", b=free[1], c=free[2])
        return ap

    c_vecs = salloc(F32, 8, 32)
    c_wdw = salloc(F32, 32, 31)
    c_lam = salloc(F32, 4)
    c_sub = salloc(F32, 2)
    c_ones = salloc(F32, 128)
    c_ident = salloc(BF16, 128)
    c_relt = salloc(F32, 16)
    c_small = salloc(F32, 16)
    acc1 = salloc(F32, NT)
    acc2 = salloc(F32, NT)
    sqtmp = salloc(F32, NT)
    rstd_b = salloc(F32, NT)
    MARK0 = M.top
    VG_PRE, VB_DW, VG_CLN, VB_CLN, VG_MPOST, VG_FPRE, VG_FPOST, VG_PPOST = range(8)

    def vec(i, ch):
        return c_vecs[:, i, ch:ch + 1]

    sem_c = P.sem("c")
    for dst, src in ((c_vecs, vecs_d.rearrange("p (a b) -> p a b", b=32)),
                     (c_wdw, wdw_d.rearrange("p (a b) -> p a b", b=31)),
                     (c_lam, lamv_d), (c_sub, subln_d), (c_ident, ident_d),
                     (c_relt[0:32], relt_d)):
        P.d("sp", lambda e, dst=dst, src=src: e.dma_start(out=dst, in_=src), sem_c)
    ev_const = (sem_c, sem_c.v)

    def bank(b, n=512, off=0):
        return ps[:, b * 512 + off:b * 512 + off + n]

    P.c("dve", lambda e: e.memset(c_ones, 1.0))
    P.c("dve", lambda e: e.tensor_tensor(out=c_small[:, 0:2], in0=c_lam[:, 0:2], in1=c_lam[:, 2:4], op=ALU.mult),
        waits=[ev_const])
    ev = P.c("pe", lambda e: e.matmul(bank(7, 2), lhsT=c_ones, rhs=c_small[:, 0:2], start=True, stop=True),
             waits=[(P.prog["dve"], P.prog["dve"].v)])
    ev = P.c("act", lambda e: e.activation(out=c_small[:, 2:4], in_=bank(7, 2), func=AF.Exp), waits=[ev])
    ev = P.c("dve", lambda e: e.tensor_tensor(out=c_small[:, 4:5], in0=c_small[:, 2:3], in1=c_small[:, 3:4], op=ALU.subtract),
             waits=[ev])
    P.c("dve", lambda e: e.tensor_scalar(out=c_small[:, 5:6], in0=c_small[:, 4:5], scalar1=LAMBDA_INIT, scalar2=-1.0,
                                         op0=ALU.add, op1=ALU.mult), waits=[ev])
    neglam = c_small[:, 5:6]

    PW = 3072
    oh_sb = salloc(F32, PW)
    g_sb = salloc(F32, PW)
    sem_g = P.sem("g")
    sem_oh = P.sem("oh")
    ev_act_prev = [None, None]
    ev_gout = None
    for pc in range(GW // PW):
        ev_oh = P.d("sp", lambda e, pc=pc: e.dma_start(out=oh_sb[0:32], in_=oh_d[:, pc * PW:(pc + 1) * PW]), sem_oh,
                    waits=[(P.prog["pe"], P.prog["pe"].v), ev_gout])
        for j in range(PW // 512):
            b = 6 + (j % 2)
            evp = P.c("pe", lambda e, b=b, j=j: e.matmul(bank(b)[0:16], lhsT=c_relt[0:32], rhs=oh_sb[0:32, j * 512:(j + 1) * 512],
                                                         start=True, stop=True),
                      waits=[ev_oh, ev_const, ev_act_prev[j % 2]])
            ev_act_prev[j % 2] = P.c("act", lambda e, b=b, j=j: e.activation(out=g_sb[0:16, j * 512:(j + 1) * 512],
                                                                            in_=bank(b)[0:16], func=AF.Copy),
                                     waits=[evp, ev_gout])
        ev_gout = P.d("sp", lambda e, pc=pc: e.dma_start(out=g_d[:, pc * PW:(pc + 1) * PW], in_=g_sb[0:16]), sem_g,
                      waits=[ev_act_prev[0], ev_act_prev[1]])
    sem_z = P.sem("z")
    for h in range(NH):
        for pc in range(4):
            w = GW // 4
            dst = bass.AP(Z_t, h * 128 * ZROW + pc * w, [[ZROW, 128], [1, w]])
            src = bass.AP(g_d.tensor, h * GW + pc * w, [[0, 128], [1, w]])
            P.d("sp", lambda e, dst=dst, src=src: e.dma_start(out=dst, in_=src), sem_z, waits=[(sem_g, sem_g.v)])
    ev_z = (sem_z, sem_z.v)
    P.barrier(engines=("pe", "act", "dve", "sp", "pool"))
    M.top = MARK0

    NS = 3
    wslots = [salloc(BF16, 32, 256) for _ in range(NS)]
    w_ld = [P.sem("wld") for _ in range(NS)]
    wstate = {"cnt": 0, "rel": [None] * NS}
    MARK1 = M.top

    def load_panel(w2d, r0, nkc, c0, ncols):
        i = wstate["cnt"]
        wstate["cnt"] += 1
        s = i % NS
        src = w2d[r0:r0 + nkc * 128, c0:c0 + ncols].rearrange("(kc p) n -> p kc n", p=128)
        ev = P.d("pool", lambda e, s=s, src=src: e.dma_start(out=wslots[s][:, 0:nkc, 0:ncols], in_=src), w_ld[s],
                 waits=[wstate["rel"][s]])
        P.dma_out.pop()
        return s, ev

    psstate = {"i": 0, "free": [[], []]}

    def linear(w2d, r0, nkc, cols, act, tiles, epi, ncols=256):
        pend = {}
        st = {"nxt": 0}

        def ensure(upto):
            while st["nxt"] < len(cols) and st["nxt"] <= upto:
                pend[st["nxt"]] = load_panel(w2d, r0, nkc, cols[st["nxt"]], ncols)
                st["nxt"] += 1

        for pi, c0 in enumerate(cols):
            ensure(pi + NS - 1)
            s, evld = pend.pop(pi)
            noc = ncols // 128
            for o in range(noc):
                slot = psstate["i"] % 2
                psstate["i"] += 1
                pst = ps[:, slot * 1536:(slot + 1) * 1536]
                waits = list(psstate["free"][slot]) + [evld]
                nmm = nkc * len(tiles)
                k = 0
                for kc in range(nkc):
                    for (t0, n, p0) in tiles:
                        k += 1
                        last = (k == nmm)
                        fn = (lambda e, pst=pst, s=s, kc=kc, o=o, t0=t0, n=n, p0=p0:
                              e.matmul(pst[:, p0:p0 + n], lhsT=wslots[s][:, kc, o * 128:(o + 1) * 128],
                                       rhs=act[:, kc, t0:t0 + n], start=(kc == 0), stop=(kc == nkc - 1)))
                        evp = P.c("pe", fn, waits=waits if k == 1 else (), signal=last)
                wstate["rel"][s] = evp
                psstate["free"][slot] = epi(c0 + o * 128, pst, evp)

    T1024 = [(0, 512, 0), (512, 512, 512)]
    T1056 = [(0, 512, 0), (512, 512, 512), (1024, 32, 1024)]
    T1024H = [(HAL, 512, 0), (HAL + 512, 512, 512)]

    class Ring:
        def __init__(self, n, dt, *free):
            self.bufs = [salloc(dt, *free) for _ in range(n)]
            self.sems = [P.sem("rg") for _ in range(n)]
            self.i = 0
            self.n = n

        def next(self):
            k = self.i % self.n
            self.i += 1
            return self.bufs[k], self.sems[k], (self.sems[k], self.sems[k].v)

    def stats_finish(acc, scale, out_rstd):
        evd = (P.prog["dve"], P.prog["dve"].v)
        for t in range(2):
            evp = P.c("pe", lambda e, t=t: e.matmul(bank(6 + t), lhsT=c_ones, rhs=acc[:, t * 512:(t + 1) * 512],
                                                     start=True, stop=True), waits=[evd])
        eva = P.c("act", lambda e: e.activation(out=out_rstd, in_=ps[:, 6 * 512:8 * 512], func=AF.Sqrt, bias=EPS, scale=scale),
                  waits=[evp])
        return P.c("dve", lambda e: e.reciprocal(out=out_rstd, in_=out_rstd), waits=[eva])

    hT = salloc(BF16, 32, NTH)
    MARK2 = M.top
    xb = [salloc(F32, NTH) for _ in range(2)]
    xsem = [P.sem("xl") for _ in range(2)]
    sqh = [salloc(F32, NTH) for _ in range(2)]
    rstd_h = salloc(F32, NTH)
    def rms_stage(xsrc):
        xfree = [None, None]
        sqfree = [None, None]
        for ch in range(32):
            b = ch % 2
            evl = P.d("sp", lambda e, b=b, ch=ch: e.dma_start(out=xb[b], in_=xsrc[ch * 128:(ch + 1) * 128, :]), xsem[b],
                      waits=[xfree[b]])
            eva = P.c("act", lambda e, b=b: e.activation(out=sqh[b], in_=xb[b], func=AF.Square), waits=[evl, sqfree[b]])
            xfree[b] = eva
            for (t0, n, p0) in T1056:
                evp = P.c("pe", lambda e, b=b, t0=t0, n=n, p0=p0, ch=ch: e.matmul(ps[:, p0:p0 + n], lhsT=c_ones, rhs=sqh[b][:, t0:t0 + n],
                                                                                  start=(ch == 0), stop=(ch == 31)),
                          waits=[eva], signal=True)
            sqfree[b] = evp
        eva = P.c("act", lambda e: e.activation(out=rstd_h, in_=ps[:, 0:NTH], func=AF.Sqrt, bias=EPS, scale=1.0 / D), waits=[evp])
        evr = P.c("dve", lambda e: e.reciprocal(out=rstd_h, in_=rstd_h), waits=[eva])
        for ch in range(32):
            b = ch % 2
            evl = P.d("sp", lambda e, b=b, ch=ch: e.dma_start(out=xb[b], in_=xsrc[ch * 128:(ch + 1) * 128, :]), xsem[b],
                      waits=[xfree[b]])
            xfree[b] = P.c("dve", lambda e, b=b, ch=ch: e.scalar_tensor_tensor(out=hT[:, ch, :], in0=xb[b], scalar=vec(VG_PRE, ch),
                                                                               in1=rstd_h, op0=ALU.mult, op1=ALU.mult),
                           waits=[evl, evr])
        P.barrier()


    stg = Ring(2, BF16, NT)

    def epi_store_bf16(dst, base):
        def epi(col, pst, evp):
            buf, sem, evfree = stg.next()
            eva = P.c("act", lambda e: e.activation(out=buf, in_=pst[:, 0:NT], func=AF.Copy), waits=[evp, evfree])
            r = col - base
            P.d("sp", lambda e: e.dma_start(out=dst[r:r + 128, :], in_=buf), sem, waits=[eva])
            return [eva]
        return epi

    vst = Ring(2, BF16, 256)
    MARKS2 = M.top
    for blk in range(NCORE + 1):
      if blk == NCORE:
        rms_stage(xT)
        break
      rms_stage(xT_all[blk * D:(blk + 1) * D, :])
      linear(w_in, 0, 32, [K0 + i * 256 for i in range(16)], hT, T1024H, epi_store_bf16(kT_all[blk * D:(blk + 1) * D, :], K0))
      for pi in range(16):
          s, evld = load_panel(w_in, 0, 32, V0 + pi * 256, 256)
          for tt in range(8):
              slot = psstate["i"] % 2
              psstate["i"] += 1
              pst = ps[:, slot * 1536:(slot + 1) * 1536]
              waits = list(psstate["free"][slot]) + [evld]
              for kc in range(32):
                  last = kc == 31
                  evp = P.c("pe", lambda e, pst=pst, s=s, kc=kc, tt=tt: e.matmul(
                      pst[:, 0:256], lhsT=hT[:, kc, HAL + tt * 128:HAL + (tt + 1) * 128], rhs=wslots[s][:, kc, 0:256],
                      start=(kc == 0), stop=(kc == 31)), waits=waits if kc == 0 else (), signal=last)
              wstate["rel"][s] = evp
              buf, sem, evfree = vst.next()
              eva = P.c("act", lambda e, buf=buf, pst=pst: e.activation(out=buf, in_=pst[:, 0:256], func=AF.Copy),
                        waits=[evp, evfree])
              P.d("sp", lambda e, buf=buf, tt=tt, pi=pi, blk=blk: e.dma_start(out=v_all[blk * NT + tt * 128:blk * NT + (tt + 1) * 128, pi * 256:(pi + 1) * 256],
                                                                       in_=buf), sem, waits=[eva])
              psstate["free"][slot] = [eva]

      P.barrier()
    ev_cc = None
    ev_ccv = None

    linear(w_in, 0, 32, [Q0 + i * 256 for i in range(16)], hT, T1024H, epi_store_bf16(qT_d, Q0))

    valb = salloc(F32, 2, NTH)
    cbuf = [salloc(F32, NTH) for _ in range(2)]
    cacc = Ring(2, F32, NT)
    cst = {"i": 0, "cfree": [None, None]}
    P.c("dve", lambda e: e.memset(acc1, 0.0))
    P.c("dve", lambda e: e.memset(acc2, 0.0))

    def epi_val(col, pst, evp):
        o = ((col - CV0) // 128) % 2
        ev = P.c("dve", lambda e: e.tensor_copy(out=valb[:, o, :], in_=pst[:, 0:NTH]), waits=[evp])
        return [ev]

    def epi_cgate(col, pst, evp):
        ch = (col - CG0) // 128
        o = ch % 2
        k = cst["i"] % 2
        cst["i"] += 1
        cb = cbuf[k]
        eva = P.c("act", lambda e: e.activation(out=cb, in_=pst[:, 0:NTH], func=AF.Sigmoid), waits=[evp, cst["cfree"][k]])
        P.c("dve", lambda e: e.tensor_tensor(out=cb, in0=cb, in1=valb[:, o, :], op=ALU.mult), waits=[eva])
        buf, sem, evfree = cacc.next()
        P.c("dve", lambda e: e.tensor_scalar(out=buf, in0=cb[:, 1:1 + NT], scalar1=c_wdw[:, ch, 0:1], scalar2=vec(VB_DW, ch),
                                             op0=ALU.mult, op1=ALU.add), waits=[evfree])
        for j in range(1, 31):
            evd = P.c("dve", lambda e, j=j: e.scalar_tensor_tensor(out=buf, in0=cb[:, 1 + j:1 + j + NT], scalar=c_wdw[:, ch, j:j + 1],
                                                                   in1=buf, op0=ALU.mult, op1=ALU.add))
        cst["cfree"][k] = evd
        P.d("sp", lambda e: e.dma_start(out=conv_d[ch * 128:(ch + 1) * 128, :], in_=buf), sem, waits=[evd])
        P.c("dve", lambda e: e.tensor_tensor(out=acc1, in0=acc1, in1=buf, op=ALU.add))
        evq = P.c("act", lambda e: e.activation(out=sqtmp, in_=buf, func=AF.Square),
                  waits=[evd, (P.prog["dve"], P.prog["dve"].v - 0)])
        P.c("dve", lambda e: e.tensor_tensor(out=acc2, in0=acc2, in1=sqtmp, op=ALU.add), waits=[evq])
        return [eva]

    for i in range(16):
        linear(w_in, 0, 32, [CV0 + i * 256], hT, T1056, epi_val)
        linear(w_in, 0, 32, [CG0 + i * 256], hT, T1056, epi_cgate)

    gst = Ring(2, F32, NT)

    def epi_gate(col, pst, evp):
        buf, sem, evfree = gst.next()
        eva = P.c("act", lambda e: e.activation(out=buf, in_=pst[:, 0:NT], func=AF.Sigmoid), waits=[evp, evfree])
        r = col - G0
        P.d("sp", lambda e: e.dma_start(out=gate_d[r:r + 128, :], in_=buf), sem, waits=[eva])
        return [eva]

    linear(w_in, 0, 32, [G0 + i * 256 for i in range(32)], hT, T1024H, epi_gate)
    P.barrier()
    M.top = MARK1

    cact = salloc(BF16, 32, NT)
    MARK3 = M.top
    mean_b = salloc(F32, NT)
    rstd_c = salloc(F32, NT)
    for t in range(2):
        P.c("pe", lambda e, t=t: e.matmul(bank(0 + t), lhsT=c_ones, rhs=acc1[:, t * 512:(t + 1) * 512], start=True, stop=True))
    for t in range(2):
        evp = P.c("pe", lambda e, t=t: e.matmul(bank(3 + t), lhsT=c_ones, rhs=acc2[:, t * 512:(t + 1) * 512], start=True, stop=True))
    P.c("dve", lambda e: e.tensor_scalar(out=mean_b, in0=ps[:, 0:NT], scalar1=1.0 / D, scalar2=None, op0=ALU.mult), waits=[evp])
    P.c("dve", lambda e: e.tensor_tensor(out=sqtmp, in0=mean_b, in1=mean_b, op=ALU.mult))
    evd = P.c("dve", lambda e: e.scalar_tensor_tensor(out=rstd_c, in0=ps[:, 1536:1536 + NT], scalar=1.0 / D, in1=sqtmp,
                                                      op0=ALU.mult, op1=ALU.subtract))
    eva = P.c("act", lambda e: e.activation(out=rstd_c, in_=rstd_c, func=AF.Sqrt, bias=EPS, scale=1.0), waits=[evd])
    evr = P.c("dve", lambda e: e.reciprocal(out=rstd_c, in_=rstd_c), waits=[eva])
    cin = [salloc(F32, NT) for _ in range(2)]
    cinsem = [P.sem("cin") for _ in range(2)]
    cfree = [None, None]
    for ch in range(32):
        b = ch % 2
        evl = P.d("sp", lambda e, b=b, ch=ch: e.dma_start(out=cin[b], in_=conv_d[ch * 128:(ch + 1) * 128, :]), cinsem[b],
                  waits=[cfree[b]])
        P.c("dve", lambda e, b=b: e.tensor_tensor(out=cin[b], in0=cin[b], in1=mean_b, op=ALU.subtract), waits=[evl, evr])
        evd = P.c("dve", lambda e, b=b: e.tensor_tensor(out=cin[b], in0=cin[b], in1=rstd_c, op=ALU.mult))
        cfree[b] = P.c("act", lambda e, b=b, ch=ch: e.activation(out=cact[:, ch, :], in_=cin[b], func=AF.Silu,
                                                                 bias=vec(VB_CLN, ch), scale=vec(VG_CLN, ch)), waits=[evd])
    P.barrier()
    M.top = MARK3

    def gate_loader(row0, bufs=None):
        gb = bufs if bufs is not None else [salloc(F32, NT) for _ in range(2)]
        gsem = [P.sem("gl") for _ in range(2)]
        st = {"free": [None, None], "ev": {}}

        def issue(ch):
            if ch >= 32:
                return
            b = ch % 2
            st["ev"][ch] = P.d("sp", lambda e: e.dma_start(out=gb[b], in_=gate_d[row0 + ch * 128:row0 + (ch + 1) * 128, :]),
                               gsem[b], waits=[st["free"][b]])
        return gb, st, issue

    gb, gstt, gissue = gate_loader(D)
    sst = Ring(2, F32, NT)
    gissue(0)

    def epi_convo(col, pst, evp):
        ch = col // 128
        b = ch % 2
        gissue(ch + 1)
        buf, sem, evfree = sst.next()
        evd = P.c("dve", lambda e: e.tensor_tensor(out=buf, in0=pst[:, 0:NT], in1=gb[b], op=ALU.mult),
                  waits=[evp, gstt["ev"][ch], evfree])
        gstt["free"][b] = evd
        P.d("sp", lambda e: e.dma_start(out=stash_d[ch * 128:(ch + 1) * 128, :], in_=buf), sem, waits=[evd])
        return [evd]

    linear(w_conv_o, 0, 32, [i * 256 for i in range(16)], cact, T1024, epi_convo)
    P.barrier(engines=("pe", "act", "dve", "sp", "pool"))
    M.top = MARK0

    Kh = [salloc(BF16, 2, SEQ) for _ in range(2)]
    Vh = [salloc(BF16, 64, 257) for _ in range(2)]
    Qh = [salloc(BF16, 2, NT) for _ in range(2)]
    kvsem = [P.sem("kv") for _ in range(2)]
    NBG = 8
    bring = [salloc(F32, NBG, 256) for _ in range(3)]
    bsem = [P.sem("bs") for _ in range(3)]
    tmpb = [salloc(F32, 2, 256) for _ in range(2)]
    Pb = [salloc(BF16, 2, 256) for _ in range(3)]
    osb = [salloc(F32, 256) for _ in range(2)]
    onb = [salloc(BF16, 256) for _ in range(2)]
    junk = salloc(F32, 256)
    rsb = salloc(F32, 8)
    oTst = [salloc(BF16, 2, NT) for _ in range(2)]
    otsem = [P.sem("ot") for _ in range(2)]
    psT = ps[:, 6 * 512:7 * 512].bitcast(BF16)
    for b in range(2):
        P.c("dve", lambda e, b=b: e.memset(Vh[b][:, :, 256:257], 1.0))
    ev_ones = (P.prog["dve"], P.prog["dve"].v)

    kv_free = [None, None]
    kv_ev = {}

    def load_head(h):
        b = h % 2
        w = [ev_cc, ev_ccv, kv_free[b]]
        for m in range(2):
            src = bass.AP(kT_all.tensor, (h * 256 + m * 128) * NT, [[NT, 128], [D * NT, NCORE], [1, NT]])
            P.d("sp", lambda e, src=src, b=b, m=m: e.dma_start(out=Kh[b][:, m, :].rearrange("p (r t) -> p r t", t=NT), in_=src),
                kvsem[b], waits=w)
        src = bass.AP(v_all.tensor, h * 256, [[D, 128], [128 * D, 64], [1, 256]])
        P.d("sp", lambda e, src=src, b=b: e.dma_start(out=Vh[b][:, :, 0:256], in_=src), kvsem[b], waits=w + [ev_ones])
        src = qT_d[h * 256:(h + 1) * 256, :].rearrange("(m p) t -> p m t", p=128)
        P.d("sp", lambda e, src=src, b=b: e.dma_start(out=Qh[b], in_=src), kvsem[b], waits=w)
        kv_ev[h] = (kvsem[b], kvsem[b].v)

    bst = {"i": 0, "free": [None, None, None]}

    def load_bias(h, qb, g):
        k = bst["i"] % 3
        bst["i"] += 1
        off = h * 128 * ZROW + (GOFF - 63 * 128) + qb * 256 + g * NBG * 128
        src = bass.AP(Z_t, off, [[GW, 128], [128, NBG], [1, 256]])
        ev = P.d("sp", lambda e, src=src, k=k: e.dma_start(out=bring[k], in_=src), bsem[k], waits=[ev_z, bst["free"][k]])
        return k, ev

    sfree = [None, None]
    tfree = [None, None]
    pfree = [None, None, None]
    accfree = [None]
    pend_tr = []
    trfree = [None]
    otfree = [None, None]
    ot_store_evs = []

    def flush_transposes():
        while pend_tr:
            fn = pend_tr.pop(0)
            fn()

    load_head(0)
    blk = 0
    for h in range(NH):
        hb = h % 2
        if h + 1 < NH:
            load_head(h + 1)
        for qb in range(4):
            biasg = {}
            biasg[0] = load_bias(h, qb, 0)
            av_q = []
            for i in range(64):
                kc = 63 - i
                g, ii = divmod(i, NBG)
                if ii == 0 and g + 1 < 64 // NBG:
                    biasg[g + 1] = load_bias(h, qb, g + 1)
                bk, bev = biasg[g]
                sb_ = i % 2
                psS = bank(4 + sb_)
                for m in range(2):
                    evs = P.c("pe", lambda e, psS=psS, m=m, kc=kc, hb=hb, qb=qb: e.matmul(
                        psS[:, m * 256:(m + 1) * 256], lhsT=Kh[hb][:, m, kc * 128:(kc + 1) * 128],
                        rhs=Qh[hb][:, m, qb * 256:(qb + 1) * 256], start=True, stop=True),
                        waits=[kv_ev[h], sfree[sb_]] if m == 0 else (), signal=(m == 1))
                tb_ = i % 2
                for m in range(2):
                    evd = P.c("dve", lambda e, psS=psS, m=m, tb_=tb_, bk=bk, ii=ii: e.scalar_tensor_tensor(
                        out=tmpb[tb_][:, m, :], in0=psS[:, m * 256:(m + 1) * 256], scalar=SCALE, in1=bring[bk][:, ii, :],
                        op0=ALU.mult, op1=ALU.add), waits=[evs, bev, tfree[tb_]] if m == 0 else ())
                sfree[sb_] = evd
                if ii == NBG - 1:
                    bst["free"][bk] = evd
                pk = i % 3
                eva = P.c("act", lambda e, pk=pk, tb_=tb_: e.activation(out=Pb[pk], in_=tmpb[tb_], func=AF.Exp),
                          waits=[evd, pfree[pk]])
                tfree[tb_] = eva

                def av(i=i, kc=kc, pk=pk, eva=eva, hb=hb):
                    n = 0
                    for m in range(2):
                        for js in range(2):
                            n += 1
                            w = [eva]
                            if i == 0:
                                w.append(accfree[0])
                            evp = P.c("pe", lambda e, m=m, js=js: e.matmul(
                                bank(m * 2 + js, 257), lhsT=Pb[pk][:, m, js * 128:(js + 1) * 128], rhs=Vh[hb][:, kc, :],
                                start=(i == 0), stop=(i == 63)), waits=w if n == 1 else (), signal=(n == 4))
                    pfree[pk] = evp
                    return evp
                av_q.append(av)
                if len(av_q) > 1:
                    ev_av = av_q.pop(0)()
                if i == 8:
                    flush_transposes()
            ev_av = av_q.pop(0)()
            if h == NH - 1 or qb == 3:
                pass
            ob = blk % 2
            blk += 1
            for js in range(2):
                k2 = (blk + js) % 2
                a0 = bank(0 + js, 257)
                a1 = bank(2 + js, 257)
                e0 = P.c("dve", lambda e, a0=a0: e.reciprocal(out=rsb[:, 0:1], in_=a0[:, 256:257]), waits=[ev_av])
                e1 = P.c("dve", lambda e, a1=a1: e.reciprocal(out=rsb[:, 1:2], in_=a1[:, 256:257]))
                e1 = P.c("dve", lambda e: e.tensor_tensor(out=rsb[:, 1:2], in0=rsb[:, 1:2], in1=neglam, op=ALU.mult), waits=[e1])
                e2 = P.c("dve", lambda e, a0=a0, k2=k2: e.tensor_scalar(out=osb[k2], in0=a0[:, 0:256], scalar1=rsb[:, 0:1],
                                                                        scalar2=None, op0=ALU.mult), waits=[e1])
                evd = P.c("dve", lambda e, a1=a1, k2=k2: e.scalar_tensor_tensor(out=osb[k2], in0=a1[:, 0:256], scalar=rsb[:, 1:2],
                                                                                in1=osb[k2], op0=ALU.mult, op1=ALU.add), waits=[e2])
                if js == 1:
                    accfree[0] = evd
                eva = P.c("act", lambda e, k2=k2, js=js: e.activation(out=junk, in_=osb[k2], func=AF.Square,
                                                                      accum_out=rsb[:, 2 + js:3 + js]), waits=[evd])
                eva = P.c("act", lambda e, js=js: e.activation(out=rsb[:, 4 + js:5 + js], in_=rsb[:, 2 + js:3 + js], func=AF.Sqrt,
                                                               bias=EPS, scale=1.0 / 256), waits=[eva])
                e3 = P.c("dve", lambda e, js=js: e.reciprocal(out=rsb[:, 6 + js:7 + js], in_=rsb[:, 4 + js:5 + js]), waits=[eva])
                evn = P.c("dve", lambda e, k2=k2, js=js: e.tensor_scalar(out=onb[k2], in0=osb[k2], scalar1=rsb[:, 6 + js:7 + js],
                                                                         scalar2=None, op0=ALU.mult), waits=[e3])

                def tr(evn=evn, k2=k2, js=js, h=h, qb=qb):
                    hb2 = h % 2
                    for half in range(2):
                        col = (js * 2 + half) * 128
                        evp = P.c("pe", lambda e, half=half, col=col: e.transpose(
                            out=psT[:, col:col + 128], in_=onb[k2][:, half * 128:(half + 1) * 128], identity=c_ident),
                            waits=[evn, trfree[0]] if half == 0 else ())
                        q0 = qb * 256 + js * 128
                        evc = P.c("dve", lambda e, half=half, col=col, q0=q0: e.tensor_scalar(
                            out=oTst[hb2][:, half, q0:q0 + 128], in0=psT[:, col:col + 128], scalar1=c_sub[:, half:half + 1],
                            scalar2=1.0 - LAMBDA_INIT, op0=ALU.mult, op1=ALU.mult), waits=[evp, otfree[hb2]])
                    trfree[0] = evc
                    if qb == 3 and js == 1:
                        dst = oT_d[h * 256:(h + 1) * 256, :].rearrange("(a p) t -> p a t", p=128)
                        ot_store_evs.append(P.d("sp", lambda e: e.dma_start(out=dst, in_=oTst[hb2]), otsem[hb2], waits=[evc]))
                        otfree[hb2] = (otsem[hb2], otsem[hb2].v)
                pend_tr.append(tr)
            if h == NH - 1 and qb == 3:
                flush_transposes()
        kv_free[hb] = ev_av
    P.barrier(engines=("pe", "act", "dve", "sp", "pool"))
    M.top = MARK1

    merged = salloc(BF16, 32, NT)
    MARK4 = M.top
    oact = salloc(BF16, 32, NT)
    sem_o = P.sem("oact")
    ev_o = P.d("sp", lambda e: e.dma_start(out=oact, in_=oT_d.rearrange("(kc p) t -> p kc t", p=128)), sem_o)
    gb2, gstt2, gissue2 = gate_loader(0, [acc1, acc2])
    stb = [sqtmp, rstd_b]
    stsem = [P.sem("stl") for _ in range(2)]
    stst = {"free": [None, None], "ev": {}}

    def stash_issue(ch):
        if ch >= 32:
            return
        b = ch % 2
        stst["ev"][ch] = P.d("sp", lambda e: e.dma_start(out=stb[b], in_=stash_d[ch * 128:(ch + 1) * 128, :]), stsem[b],
                             waits=[stst["free"][b]])
    gissue2(0)
    stash_issue(0)

    def epi_attno(col, pst, evp):
        ch = col // 128
        b = ch % 2
        gissue2(ch + 1)
        stash_issue(ch + 1)
        evd = P.c("dve", lambda e: e.tensor_tensor(out=gb2[b], in0=pst[:, 0:NT], in1=gb2[b], op=ALU.mult),
                  waits=[evp, gstt2["ev"][ch]])
        ev2 = P.c("dve", lambda e: e.tensor_tensor(out=merged[:, ch, :], in0=gb2[b], in1=stb[b], op=ALU.add),
                  waits=[stst["ev"][ch]])
        gstt2["free"][b] = ev2
        stst["free"][b] = ev2
        return [evd]

    P.op("pe", None, [ev_o], ())
    linear(w_attn_o, 0, 32, [i * 256 for i in range(16)], oact, T1024, epi_attno)
    P.barrier()
    M.top = MARK4

    mst = Ring(2, F32, NT)
    P.c("dve", lambda e: e.memset(acc2, 0.0))

    def epi_store_stats(dst):
        def epi(col, pst, evp):
            ch = col // 128
            buf, sem, evfree = mst.next()
            eva = P.c("act", lambda e: e.activation(out=buf, in_=pst[:, 0:NT], func=AF.Copy), waits=[evp, evfree])
            P.d("sp", lambda e: e.dma_start(out=dst[ch * 128:(ch + 1) * 128, :], in_=buf), sem, waits=[eva])
            evq = P.c("act", lambda e: e.activation(out=sqtmp, in_=buf, func=AF.Square),
                      waits=[(P.prog["dve"], P.prog["dve"].v)])
            P.c("dve", lambda e: e.tensor_tensor(out=acc2, in0=acc2, in1=sqtmp, op=ALU.add), waits=[evq])
            return [eva]
        return epi

    linear(w_out, 0, 32, [i * 256 for i in range(16)], merged, T1024, epi_store_stats(mix_d))
    P.barrier()
    M.top = MARK1
    h2T = salloc(BF16, 32, NT)
    MARK5 = M.top

    def resid_pass(src_a, src_b, b_is_x, gidx, rstd, dst, acc=None, bf_out=None):
        ia = [salloc(F32, NT) for _ in range(2)]
        ib = [salloc(F32, NT) for _ in range(2)]
        sa = [P.sem("ia") for _ in range(2)]
        sb_ = [P.sem("ib") for _ in range(2)]
        so = [P.sem("io") for _ in range(2)]
        fa = [None, None]
        evs = []
        if acc is not None:
            P.c("dve", lambda e: e.memset(acc, 0.0))
        for ch in range(32):
            b = ch % 2
            rows = slice(ch * 128, (ch + 1) * 128)
            e1 = P.d("sp", lambda e, b=b, rows=rows: e.dma_start(out=ia[b], in_=src_a[rows, :]), sa[b], waits=[fa[b]])
            srcb = src_b[rows, HAL:HAL + NT] if b_is_x else src_b[rows, :]
            e2 = P.d("sp", lambda e, b=b, srcb=srcb: e.dma_start(out=ib[b], in_=srcb), sb_[b],
                     waits=[fa[b], (so[b], so[b].v)])
            P.c("dve", lambda e, b=b, ch=ch: e.scalar_tensor_tensor(out=ia[b], in0=ia[b], scalar=vec(gidx, ch), in1=rstd,
                                                                    op0=ALU.mult, op1=ALU.mult), waits=[e1])
            evd = P.c("dve", lambda e, b=b: e.tensor_tensor(out=ib[b], in0=ib[b], in1=ia[b], op=ALU.add), waits=[e2])
            evs.append(P.d("sp", lambda e, b=b, rows=rows: e.dma_start(out=dst[rows, :], in_=ib[b]), so[b], waits=[evd]))
            last = evd
            if acc is not None:
                evq = P.c("act", lambda e, b=b: e.activation(out=sqtmp, in_=ib[b], func=AF.Square),
                          waits=[evd, (P.prog["dve"], P.prog["dve"].v)])
                last = P.c("dve", lambda e: e.tensor_tensor(out=acc, in0=acc, in1=sqtmp, op=ALU.add), waits=[evq])
            if bf_out is not None:
                last2 = P.c("act", lambda e, b=b, ch=ch: e.activation(out=bf_out[:, ch, :], in_=ib[b], func=AF.Copy), waits=[evd])
                fa[b] = last2
                P.op("dve", None, [last2], ())
            else:
                fa[b] = last
        return evs

    stats_finish(acc2, 1.0 / D, rstd_b)
    resid_pass(mix_d, xT, True, VG_MPOST, rstd_b, x1_d, acc=acc1)
    P.barrier()
    M.top = MARK5
    stats_finish(acc1, 1.0 / D, rstd_b)
    xl = [salloc(F32, NT) for _ in range(2)]
    xls = [P.sem("xl2") for _ in range(2)]
    xlf = [None, None]
    for ch in range(32):
        b = ch % 2
        evl = P.d("sp", lambda e, b=b, ch=ch: e.dma_start(out=xl[b], in_=x1_d[ch * 128:(ch + 1) * 128, :]), xls[b], waits=[xlf[b]])
        xlf[b] = P.c("dve", lambda e, b=b, ch=ch: e.scalar_tensor_tensor(out=h2T[:, ch, :], in0=xl[b], scalar=vec(VG_FPRE, ch),
                                                                         in1=rstd_b, op0=ALU.mult, op1=ALU.mult), waits=[evl])
    P.barrier()
    M.top = MARK5

    uT = salloc(BF16, 32, NT)
    rtmp = [acc1, rstd_b]
    fin = [salloc(F32, NT) for _ in range(2)]
    finsem = [P.sem("fin") for _ in range(2)]
    fstsem = [P.sem("fst") for _ in range(2)]
    ffn_store = {}
    rst = {"i": 0, "free": [None, None]}
    P.c("dve", lambda e: e.memset(acc2, 0.0))
    for qq in range(4):
        def epi_up(col, pst, evp, qq=qq):
            ch = (col - qq * D) // 128
            k = rst["i"] % 2
            rst["i"] += 1
            evd = P.c("dve", lambda e: e.tensor_scalar(out=rtmp[k], in0=pst[:, 0:NT], scalar1=0.0, scalar2=None, op0=ALU.max),
                      waits=[evp, rst["free"][k]])
            rst["free"][k] = P.c("act", lambda e: e.activation(out=uT[:, ch, :], in_=rtmp[k], func=AF.Square), waits=[evd])
            return [evd]

        linear(w_up, 0, 32, [qq * D + i * 256 for i in range(16)], h2T, T1024, epi_up)
        P.op("pe", None, [(P.prog["act"], P.prog["act"].v)], ())
        fst8 = {"ev": {}}

        def fin_issue(ch, qq=qq):
            if ch >= 32 or qq == 0:
                return
            b = ch % 2
            fst8["ev"][ch] = P.d("sp", lambda e: e.dma_start(out=fin[b], in_=ffn_d[ch * 128:(ch + 1) * 128, :]), finsem[b],
                                 waits=[(fstsem[b], fstsem[b].v), ffn_store[ch]])
        fin_issue(0)

        def epi_down(col, pst, evp, qq=qq):
            ch = col // 128
            b = ch % 2
            fin_issue(ch + 1)
            buf = fin[b]
            if qq == 0:
                ev = P.c("act", lambda e: e.activation(out=buf, in_=pst[:, 0:NT], func=AF.Copy),
                         waits=[evp, (fstsem[b], fstsem[b].v)])
            else:
                ev = P.c("dve", lambda e: e.tensor_tensor(out=buf, in0=pst[:, 0:NT], in1=buf, op=ALU.add),
                         waits=[evp, fst8["ev"][ch]])
            ffn_store[ch] = P.d("sp", lambda e: e.dma_start(out=ffn_d[ch * 128:(ch + 1) * 128, :], in_=buf), fstsem[b], waits=[ev])
            if qq == 3:
                evq = P.c("act", lambda e: e.activation(out=sqtmp, in_=buf, func=AF.Square),
                          waits=[ev, (P.prog["dve"], P.prog["dve"].v)])
                P.c("dve", lambda e: e.tensor_tensor(out=acc2, in0=acc2, in1=sqtmp, op=ALU.add), waits=[evq])
            return [ev]

        linear(w_down, qq * D, 32, [i * 256 for i in range(16)], uT, T1024, epi_down)
    P.barrier()
    M.top = MARK1

    x2T = salloc(BF16, 32, NT)
    MARK6 = M.top
    stats_finish(acc2, 1.0 / D, rstd_b)
    resid_pass(ffn_d, x1_d, False, VG_FPOST, rstd_b, x2_d, acc=None, bf_out=x2T)
    P.barrier()
    M.top = MARK6

    pbf = salloc(BF16, 2, NT)
    wpp = salloc(BF16, 2, D)
    sem_p = P.sem("ple")
    P.d("pool", lambda e: e.dma_start(out=pbf, in_=pT.rearrange("(kc p) t -> p kc t", p=128)), sem_p)
    P.d("pool", lambda e: e.dma_start(out=wpp, in_=w_ple_proj.rearrange("(kc p) n -> p kc n", p=128)), sem_p)
    ev_p = (sem_p, sem_p.v)
    P.dma_out = [x for x in P.dma_out if x[0] is not sem_p]
    sgt = [salloc(F32, NT) for _ in range(2)]
    tst = Ring(2, F32, NT)
    P.c("dve", lambda e: e.memset(acc2, 0.0))
    pst8 = {"efree": None, "i": 0, "sfree": [None, None]}

    def epi_ple(col, pst, evp):
        ch = col // 128
        k = pst8["i"] % 2
        pst8["i"] += 1
        n = 0
        for kc in range(2):
            for t in range(2):
                n += 1
                evE = P.c("pe", lambda e, kc=kc, t=t: e.matmul(bank(6 + t), lhsT=wpp[:, kc, ch * 128:(ch + 1) * 128],
                                                                rhs=pbf[:, kc, t * 512:(t + 1) * 512], start=(kc == 0), stop=(kc == 1)),
                          waits=[ev_p, pst8["efree"]] if n == 1 else (), signal=(n == 4))
        eva = P.c("act", lambda e: e.activation(out=sgt[k], in_=pst[:, 0:NT], func=AF.Sigmoid), waits=[evp, pst8["sfree"][k]])
        buf, sem, evfree = tst.next()
        evd = P.c("dve", lambda e: e.tensor_tensor(out=buf, in0=ps[:, 6 * 512:8 * 512], in1=sgt[k], op=ALU.mult),
                  waits=[evE, eva, evfree])
        pst8["efree"] = evd
        pst8["sfree"][k] = evd
        P.d("sp", lambda e: e.dma_start(out=t_d[ch * 128:(ch + 1) * 128, :], in_=buf), sem, waits=[evd])
        evq = P.c("act", lambda e: e.activation(out=sqtmp, in_=buf, func=AF.Square), waits=[evd, (P.prog["dve"], P.prog["dve"].v)])
        P.c("dve", lambda e: e.tensor_tensor(out=acc2, in0=acc2, in1=sqtmp, op=ALU.add), waits=[evq])
        return [eva]

    linear(w_ple_gate, 0, 32, [i * 256 for i in range(16)], x2T, T1024, epi_ple)
    P.barrier()
    stats_finish(acc2, 1.0 / D, rstd_b)
    out_evs = resid_pass(t_d, x2_d, False, VG_PPOST, rstd_b, outT)
    P.barrier(engines=("pe", "act", "dve", "sp", "pool"))
    for nm, src, eng in (("stash0", stash_d[0:128, :], "sp"), ("stash31", stash_d[3968:4096, :], "sp"),
                         ("mix0", mix_d[0:128, :], "sp"), ("mix31", mix_d[3968:4096, :], "sp"),
                         ("x1_0", x1_d[0:128, :], "sp"), ("ffn0", ffn_d[0:128, :], "sp"), ("ffn31", ffn_d[3968:4096, :], "sp"),
                         ("x2_0", x2_d[0:128, :], "sp"), ("t0", t_d[0:128, :], "sp"), ("t31", t_d[3968:4096, :], "sp")):
        dump(nm, src, NT, eng=eng)
    DBG_NAMES[:] = dbg_names

    with nc.Block() as block:
        P.emit(block)
    return nc


def _rel_bucket_np(rel):
    nb = 16
    me = 8
    ret = (rel > 0).astype(np.int32) * nb
    n = np.abs(rel)
    nf = np.maximum(n, 1).astype(np.float32)
    large = me + (np.log(nf / np.float32(me)) / np.float32(math.log(128 / me)) * np.float32(nb - me)).astype(np.int32)
    large = np.minimum(large, nb - 1)
    return ret + np.where(n < me, n, large)


_NC_CACHE = {}


def kernel(x, p, positions, rel_table, mix_pre_g, w_in, lambda_q1, lambda_k1, lambda_q2, lambda_k2, subln_g,
           w_attn_o, w_dw, b_dw, conv_ln_g, conv_ln_b, w_conv_o, w_out, mix_post_g, ffn_pre_g, w_up, w_down,
           ffn_post_g, w_ple_gate, w_ple_proj, ple_post_g):
    f32 = np.float32
    x = np.asarray(x, f32)[0]
    pp = np.asarray(p, f32)[0, 0]
    xTfull = np.zeros((D, SEQ + 2 * HAL), f32)
    xTfull[:, HAL:HAL + SEQ] = x.T

    def v128(v):
        return np.asarray(v, f32).reshape(32, 128).T

    vecs = np.stack([v128(mix_pre_g[0]), v128(b_dw[0]), v128(conv_ln_g[0]), v128(conv_ln_b[0]), v128(mix_post_g[0]),
                     v128(ffn_pre_g[0]), v128(ffn_post_g[0]), v128(ple_post_g[0])], axis=1)
    wdw = np.asarray(w_dw, f32)[0, :, 0, :]
    wdw = wdw.T.reshape(32, 128, 31).transpose(1, 0, 2)
    lamv = np.stack([np.asarray(lambda_q1, f32)[0], np.asarray(lambda_q2, f32)[0],
                     np.asarray(lambda_k1, f32)[0], np.asarray(lambda_k2, f32)[0]], axis=1)
    subl = np.asarray(subln_g, f32)[0].reshape(2, 128).T
    ident = np.eye(128, dtype=f32).astype(ml_dtypes.bfloat16)
    common = {
        "vecs": np.ascontiguousarray(vecs.reshape(128, 256)),
        "wdw": np.ascontiguousarray(wdw.reshape(128, 32 * 31)),
        "lamv": np.ascontiguousarray(lamv),
        "subln": np.ascontiguousarray(subl),
        "relt": np.ascontiguousarray(np.asarray(rel_table, f32)),
        "ident": ident,
        "w_in": np.asarray(w_in, f32)[0], "w_attn_o": np.asarray(w_attn_o, f32)[0],
        "w_conv_o": np.asarray(w_conv_o, f32)[0], "w_out": np.asarray(w_out, f32)[0],
        "w_up": np.asarray(w_up, f32)[0], "w_down": np.asarray(w_down, f32)[0],
        "w_ple_gate": np.asarray(w_ple_gate, f32)[0], "w_ple_proj": np.asarray(w_ple_proj, f32)[0],
    }
    xT_all = np.ascontiguousarray(np.concatenate([xTfull[:, b * NT:b * NT + NTH] for b in range(NCORE)], axis=0))
    common["xT_all"] = xT_all
    in_maps = []
    ii = np.arange(GW, dtype=np.int64)
    for c in range(NCORE):
        rel = (GOFF - ii - c * NT).astype(np.int32)
        bk = _rel_bucket_np(rel)
        oh = (bk[None, :] == np.arange(32)[:, None]).astype(f32)
        m = dict(common)
        m["xT"] = np.ascontiguousarray(xTfull[:, c * NT:c * NT + NTH])
        m["pT"] = np.ascontiguousarray(pp[c * NT:(c + 1) * NT].T)
        m["oh"] = oh
        in_maps.append(m)
    if "nc" not in _NC_CACHE:
        _NC_CACHE["nc"] = build()
    res = run_bass_kernel_spmd(_NC_CACHE["nc"], in_maps, core_ids=list(range(NCORE)))
    if DEBUG:
        _NC_CACHE["res"] = res
    out = np.concatenate([np.asarray(r["outT"], f32).T for r in res.results], axis=0)
    return out[None].astype(f32)
```

```python
import math
import numpy as np
import ml_dtypes
import concourse.bass as bass
import concourse.mybir as mybir
from concourse.bass_utils import run_bass_kernel_spmd

F32 = mybir.dt.float32
BF16 = mybir.dt.bfloat16
U8 = mybir.dt.uint8
AF = mybir.ActivationFunctionType
ALU = mybir.AluOpType

NCORE = 8
SEQ = 8192
NT = 1024
HAL = 16
NTH = NT + 2 * HAL
D = 4096
DFF = 16384
NH = 16
EPS = 1e-6
LAMBDA_INIT = 0.8 - 0.6 * math.exp(0.0)
SCALE = 128 ** -0.5
ARENA = 211968
GW = 9216
GOFF = 8191
ZROW = GW + 1
DEBUG = False
NDBG = 24
DBG_NAMES = []

Q0, K0, V0, CV0, CG0, G0 = 0, 4096, 8192, 12288, 16384, 20480


def _dsize(dt):
    return 4 if dt == F32 else (2 if dt == BF16 else 1)


class Sem:
    def __init__(self, nc, name):
        self.h = nc.alloc_semaphore(name)
        self.v = 0


class Prog:
    ENG = ("pe", "act", "dve", "pool", "sp")

    def __init__(self, nc):
        self.nc = nc
        self.q = {k: [] for k in self.ENG}
        self.waited = {}
        self.all_sems = []
        self.n = 0
        self.prog = {k: self.sem("prog_" + k) for k in ("pe", "act", "dve", "pool")}
        self.dma_out = []

    def sem(self, name):
        self.n += 1
        sm = Sem(self.nc, "%s_%d" % (name, self.n))
        self.all_sems.append(sm)
        return sm

    def op(self, eng, fn, waits=(), incs=()):
        ws = []
        for w in waits:
            if w is None:
                continue
            s, v = w
            if v <= 0:
                continue
            key = (eng, id(s))
            if self.waited.get(key, 0) >= v:
                continue
            self.waited[key] = v
            ws.append((s.h, v))
        ii = []
        for s, a in incs:
            s.v += a
            ii.append((s.h, a))
        self.q[eng].append((fn, ws, ii))

    def c(self, eng, fn, waits=(), incs=(), signal=True):
        incs = list(incs)
        if signal:
            incs.append((self.prog[eng], 1))
        self.op(eng, fn, waits, incs)
        return (self.prog[eng], self.prog[eng].v)

    def d(self, eng, fn, sem, waits=()):
        self.op(eng, fn, waits, [(sem, 16)])
        ev = (sem, sem.v)
        self.dma_out.append(ev)
        return ev

    def barrier(self, engines=("pe", "act", "dve", "sp", "pool")):
        evs = [(self.prog[k], self.prog[k].v) for k in ("pe", "act", "dve")]
        seen = {}
        for s, v in self.dma_out:
            seen[id(s)] = (s, max(v, seen.get(id(s), (s, 0))[1]))
        evs += list(seen.values())
        self.dma_out = []
        for e in engines:
            self.op(e, None, evs, ())

    def emit(self, block):
        def mk(qn):
            def body(e):
                for fn, ws, ii in self.q[qn]:
                    for h, v in ws:
                        e.wait_ge(h, v)
                    if fn is None:
                        continue
                    ins = fn(e)
                    for h, a in ii:
                        ins = ins.then_inc(h, a)
            return body
        block.tensor(mk("pe"))
        block.scalar(mk("act"))
        block.vector(mk("dve"))
        block.gpsimd(mk("pool"))
        block.sync(mk("sp"))


def build():
    nc = bass.Bass("TRN2", target_bir_lowering=False)
    P = Prog(nc)

    def din(name, shape, dt=F32):
        return nc.dram_tensor(name, list(shape), dt, kind="ExternalInput").ap()

    def dtmp(name, shape, dt=F32):
        return nc.dram_tensor(name, list(shape), dt).ap()

    xT = din("xT", [D, NTH])
    pT = din("pT", [256, NT])
    vecs_d = din("vecs", [128, 8 * 32])
    wdw_d = din("wdw", [128, 32 * 31])
    lamv_d = din("lamv", [128, 4])
    subln_d = din("subln", [128, 2])
    relt_d = din("relt", [32, 16])
    oh_d = din("oh", [32, GW])
    ident_d = din("ident", [128, 128], BF16)
    w_in = din("w_in", [D, 28672])
    w_attn_o = din("w_attn_o", [D, D])
    w_conv_o = din("w_conv_o", [D, D])
    w_out = din("w_out", [D, D])
    w_up = din("w_up", [D, DFF])
    w_down = din("w_down", [DFF, D])
    w_ple_gate = din("w_ple_gate", [D, D])
    w_ple_proj = din("w_ple_proj", [256, D])
    outT = nc.dram_tensor("outT", [D, NT], F32, kind="ExternalOutput").ap()

    qT_d = dtmp("qT_d", [D, NT], BF16)
    kT_loc = dtmp("kT_loc", [D, NT], BF16)
    kT_all = dtmp("kT_all", [NCORE * D, NT], BF16)
    v_loc = dtmp("v_loc", [NT, D], BF16)
    v_all = dtmp("v_all", [SEQ, D], BF16)
    gate_d = dtmp("gate_d", [2 * D, NT])
    conv_d = dtmp("conv_d", [D, NT])
    stash_d = dtmp("stash_d", [D, NT])
    oT_d = dtmp("oT_d", [D, NT], BF16)
    mix_d = dtmp("mix_d", [D, NT])
    x1_d = dtmp("x1_d", [D, NT])
    ffn_d = dtmp("ffn_d", [D, NT])
    x2_d = dtmp("x2_d", [D, NT])
    t_d = dtmp("t_d", [D, NT])
    g_d = dtmp("g_d", [NH, GW])
    Z_t = nc.dram_tensor("Z_d", [NH * 128, ZROW], F32)

    dbg_names = []
    if DEBUG:
        dbg = nc.dram_tensor("dbg", [NDBG * 128, NT], F32, kind="ExternalOutput").ap()
        sem_dbg = P.sem("dbg")

    def dump(name, ap, n, eng="sp"):
        if not DEBUG:
            return
        i = len(dbg_names)
        assert i < NDBG
        dbg_names.append(name)
        P.barrier(engines=("pe", "act", "dve", "sp", "pool"))
        P.d(eng, lambda e: e.dma_start(out=dbg[i * 128:(i + 1) * 128, 0:n], in_=ap), sem_dbg)
        P.barrier(engines=("pe", "act", "dve", "sp", "pool"))

    arena = nc.alloc_sbuf_tensor("arena", [128, ARENA], U8)
    ps = nc.alloc_psum_tensor("ps", [128, 4096], F32)

    class M:
        top = 0

    def salloc(dt, *free):
        n = int(np.prod(free)) * _dsize(dt)
        off = M.top
        M.top += (n + 63) // 64 * 64
        assert M.top <= ARENA, ("SBUF overflow", M.top)
        ap = arena[:, off:off + n].bitcast(dt)
        if len(free) == 2:
            ap = ap.rearrange("p (a b) -> p a b", b=free[1])
        elif len(free) == 3:
            ap = ap.rearrange("p (a b c) -> p a b c", b=free[1], c=free[2])
        return ap

    c_vecs = salloc(F32, 8, 32)
    c_wdw = salloc(F32, 32, 31)
    c_lam = salloc(F32, 4)
    c_sub = salloc(F32, 2)
    c_ones = salloc(F32, 128)
    c_ident = salloc(BF16, 128)
    c_relt = salloc(F32, 16)
    c_small = salloc(F32, 16)
    acc1 = salloc(F32, NT)
    acc2 = salloc(F32, NT)
    sqtmp = salloc(F32, NT)
    rstd_b = salloc(F32, NT)
    MARK0 = M.top
    VG_PRE, VB_DW, VG_CLN, VB_CLN, VG_MPOST, VG_FPRE, VG_FPOST, VG_PPOST = range(8)

    def vec(i, ch):
        return c_vecs[:, i, ch:ch + 1]

    sem_c = P.sem("c")
    for dst, src in ((c_vecs, vecs_d.rearrange("p (a b) -> p a b", b=32)),
                     (c_wdw, wdw_d.rearrange("p (a b) -> p a b", b=31)),
                     (c_lam, lamv_d), (c_sub, subln_d), (c_ident, ident_d),
                     (c_relt[0:32], relt_d)):
        P.d("sp", lambda e, dst=dst, src=src: e.dma_start(out=dst, in_=src), sem_c)
    ev_const = (sem_c, sem_c.v)

    def bank(b, n=512, off=0):
        return ps[:, b * 512 + off:b * 512 + off + n]

    P.c("dve", lambda e: e.memset(c_ones, 1.0))
    P.c("dve", lambda e: e.tensor_tensor(out=c_small[:, 0:2], in0=c_lam[:, 0:2], in1=c_lam[:, 2:4], op=ALU.mult),
        waits=[ev_const])
    ev = P.c("pe", lambda e: e.matmul(bank(7, 2), lhsT=c_ones, rhs=c_small[:, 0:2], start=True, stop=True),
             waits=[(P.prog["dve"], P.prog["dve"].v)])
    ev = P.c("act", lambda e: e.activation(out=c_small[:, 2:4], in_=bank(7, 2), func=AF.Exp), waits=[ev])
    ev = P.c("dve", lambda e: e.tensor_tensor(out=c_small[:, 4:5], in0=c_small[:, 2:3], in1=c_small[:, 3:4], op=ALU.subtract),
             waits=[ev])
    P.c("dve", lambda e: e.tensor_scalar(out=c_small[:, 5:6], in0=c_small[:, 4:5], scalar1=LAMBDA_INIT, scalar2=-1.0,
                                         op0=ALU.add, op1=ALU.mult), waits=[ev])
    neglam = c_small[:, 5:6]

    PW = 3072
    oh_sb = salloc(F32, PW)
    g_sb = salloc(F32, PW)
    sem_g = P.sem("g")
    sem_oh = P.sem("oh")
    ev_act_prev = [None, None]
    ev_gout = None
    for pc in range(GW // PW):
        ev_oh = P.d("sp", lambda e, pc=pc: e.dma_start(out=oh_sb[0:32], in_=oh_d[:, pc * PW:(pc + 1) * PW]), sem_oh,
                    waits=[(P.prog["pe"], P.prog["pe"].v), ev_gout])
        for j in range(PW // 512):
            b = 6 + (j % 2)
            evp = P.c("pe", lambda e, b=b, j=j: e.matmul(bank(b)[0:16], lhsT=c_relt[0:32], rhs=oh_sb[0:32, j * 512:(j + 1) * 512],
                                                         start=True, stop=True),
                      waits=[ev_oh, ev_const, ev_act_prev[j % 2]])
            ev_act_prev[j % 2] = P.c("act", lambda e, b=b, j=j: e.activation(out=g_sb[0:16, j * 512:(j + 1) * 512],
                                                                            in_=bank(b)[0:16], func=AF.Copy),
                                     waits=[evp, ev_gout])
        ev_gout = P.d("sp", lambda e, pc=pc: e.dma_start(out=g_d[:, pc * PW:(pc + 1) * PW], in_=g_sb[0:16]), sem_g,
                      waits=[ev_act_prev[0], ev_act_prev[1]])
    sem_z = P.sem("z")
    for h in range(NH):
        for pc in range(4):
            w = GW // 4
            dst = bass.AP(Z_t, h * 128 * ZROW + pc * w, [[ZROW, 128], [1, w]])
            src = bass.AP(g_d.tensor, h * GW + pc * w, [[0, 128], [1, w]])
            P.d("sp", lambda e, dst=dst, src=src: e.dma_start(out=dst, in_=src), sem_z, waits=[(sem_g, sem_g.v)])
    ev_z = (sem_z, sem_z.v)
    P.barrier(engines=("pe", "act", "dve", "sp", "pool"))
    M.top = MARK0

    NS = 3
    wslots = [salloc(BF16, 32, 256) for _ in range(NS)]
    w_ld = [P.sem("wld") for _ in range(NS)]
    wstate = {"cnt": 0, "rel": [None] * NS}
    MARK1 = M.top

    def load_panel(w2d, r0, nkc, c0, ncols):
        i = wstate["cnt"]
        wstate["cnt"] += 1
        s = i % NS
        src = w2d[r0:r0 + nkc * 128, c0:c0 + ncols].rearrange("(kc p) n -> p kc n", p=128)
        ev = P.d("pool", lambda e, s=s, src=src: e.dma_start(out=wslots[s][:, 0:nkc, 0:ncols], in_=src), w_ld[s],
                 waits=[wstate["rel"][s]])
        P.dma_out.pop()
        return s, ev

    psstate = {"i": 0, "free": [[], []]}

    def linear(w2d, r0, nkc, cols, act, tiles, epi, ncols=256):
        pend = {}
        st = {"nxt": 0}

        def ensure(upto):
            while st["nxt"] < len(cols) and st["nxt"] <= upto:
                pend[st["nxt"]] = load_panel(w2d, r0, nkc, cols[st["nxt"]], ncols)
                st["nxt"] += 1

        for pi, c0 in enumerate(cols):
            ensure(pi + NS - 1)
            s, evld = pend.pop(pi)
            noc = ncols // 128
            for o in range(noc):
                slot = psstate["i"] % 2
                psstate["i"] += 1
                pst = ps[:, slot * 1536:(slot + 1) * 1536]
                waits = list(psstate["free"][slot]) + [evld]
                nmm = nkc * len(tiles)
                k = 0
                for kc in range(nkc):
                    for (t0, n, p0) in tiles:
                        k += 1
                        last = (k == nmm)
                        fn = (lambda e, pst=pst, s=s, kc=kc, o=o, t0=t0, n=n, p0=p0:
                              e.matmul(pst[:, p0:p0 + n], lhsT=wslots[s][:, kc, o * 128:(o + 1) * 128],
                                       rhs=act[:, kc, t0:t0 + n], start=(kc == 0), stop=(kc == nkc - 1)))
                        evp = P.c("pe", fn, waits=waits if k == 1 else (), signal=last)
                wstate["rel"][s] = evp
                psstate["free"][slot] = epi(c0 + o * 128, pst, evp)

    T1024 = [(0, 512, 0), (512, 512, 512)]
    T1056 = [(0, 512, 0), (512, 512, 512), (1024, 32, 1024)]
    T1024H = [(HAL, 512, 0), (HAL + 512, 512, 512)]

    class Ring:
        def __init__(self, n, dt, *free):
            self.bufs = [salloc(dt, *free) for _ in range(n)]
            self.sems = [P.sem("rg") for _ in range(n)]
            self.i = 0
            self.n = n

        def next(self):
            k = self.i % self.n
            self.i += 1
            return self.bufs[k], self.sems[k], (self.sems[k], self.sems[k].v)

    def stats_finish(acc, scale, out_rstd):
        evd = (P.prog["dve"], P.prog["dve"].v)
        for t in range(2):
            evp = P.c("pe", lambda e, t=t: e.matmul(bank(6 + t), lhsT=c_ones, rhs=acc[:, t * 512:(t + 1) * 512],
                                                     start=True, stop=True), waits=[evd])
        eva = P.c("act", lambda e: e.activation(out=out_rstd, in_=ps[:, 6 * 512:8 * 512], func=AF.Sqrt, bias=EPS, scale=scale),
                  waits=[evp])
        return P.c("dve", lambda e: e.reciprocal(out=out_rstd, in_=out_rstd), waits=[eva])

    hT = salloc(BF16, 32, NTH)
    MARK2 = M.top
    xb = [salloc(F32, NTH) for _ in range(2)]
    xsem = [P.sem("xl") for _ in range(2)]
    sqh = [salloc(F32, NTH) for _ in range(2)]
    rstd_h = salloc(F32, NTH)
    def rms_stage(xsrc):
        xfree = [None, None]
        sqfree = [None, None]
        for ch in range(32):
            b = ch % 2
            evl = P.d("sp", lambda e, b=b, ch=ch: e.dma_start(out=xb[b], in_=xsrc[ch * 128:(ch + 1) * 128, :]), xsem[b],
                      waits=[xfree[b]])
            eva = P.c("act", lambda e, b=b: e.activation(out=sqh[b], in_=xb[b], func=AF.Square), waits=[evl, sqfree[b]])
            xfree[b] = eva
            for (t0, n, p0) in T1056:
                evp = P.c("pe", lambda e, b=b, t0=t0, n=n, p0=p0, ch=ch: e.matmul(ps[:, p0:p0 + n], lhsT=c_ones, rhs=sqh[b][:, t0:t0 + n],
                                                                                  start=(ch == 0), stop=(ch == 31)),
                          waits=[eva], signal=True)
            sqfree[b] = evp
        eva = P.c("act", lambda e: e.activation(out=rstd_h, in_=ps[:, 0:NTH], func=AF.Sqrt, bias=EPS, scale=1.0 / D), waits=[evp])
        evr = P.c("dve", lambda e: e.reciprocal(out=rstd_h, in_=rstd_h), waits=[eva])
        for ch in range(32):
            b = ch % 2
            evl = P.d("sp", lambda e, b=b, ch=ch: e.dma_start(out=xb[b], in_=xsrc[ch * 128:(ch + 1) * 128, :]), xsem[b],
                      waits=[xfree[b]])
            xfree[b] = P.c("dve", lambda e, b=b, ch=ch: e.scalar_tensor_tensor(out=hT[:, ch, :], in0=xb[b], scalar=vec(VG_PRE, ch),
                                                                               in1=rstd_h, op0=ALU.mult, op1=ALU.mult),
                           waits=[evl, evr])
        P.barrier()


    stg = Ring(2, BF16, NT)

    def epi_store_bf16(dst, base):
        def epi(col, pst, evp):
            buf, sem, evfree = stg.next()
            eva = P.c("act", lambda e: e.activation(out=buf, in_=pst[:, 0:NT], func=AF.Copy), waits=[evp, evfree])
            r = col - base
            P.d("sp", lambda e: e.dma_start(out=dst[r:r + 128, :], in_=buf), sem, waits=[eva])
            return [eva]
        return epi

    vst = Ring(2, BF16, 256)
    rms_stage(xT)
    linear(w_in, 0, 32, [K0 + i * 256 for i in range(16)], hT, T1024H, epi_store_bf16(kT_loc, K0))
    for pi in range(16):
        s, evld = load_panel(w_in, 0, 32, V0 + pi * 256, 256)
        for tt in range(8):
            slot = psstate["i"] % 2
            psstate["i"] += 1
            pst = ps[:, slot * 1536:(slot + 1) * 1536]
            waits = list(psstate["free"][slot]) + [evld]
            for kc in range(32):
                last = kc == 31
                evp = P.c("pe", lambda e, pst=pst, s=s, kc=kc, tt=tt: e.matmul(
                    pst[:, 0:256], lhsT=hT[:, kc, HAL + tt * 128:HAL + (tt + 1) * 128], rhs=wslots[s][:, kc, 0:256],
                    start=(kc == 0), stop=(kc == 31)), waits=waits if kc == 0 else (), signal=last)
            wstate["rel"][s] = evp
            buf, sem, evfree = vst.next()
            eva = P.c("act", lambda e, buf=buf, pst=pst: e.activation(out=buf, in_=pst[:, 0:256], func=AF.Copy),
                      waits=[evp, evfree])
            P.d("sp", lambda e, buf=buf, tt=tt, pi=pi: e.dma_start(out=v_loc[tt * 128:(tt + 1) * 128, pi * 256:(pi + 1) * 256],
                                                                     in_=buf), sem, waits=[eva])
            psstate["free"][slot] = [eva]
    kv_store_evs = [(s_, s_.v) for s_ in stg.sems + vst.sems]
    sem_cck = P.sem("cck")
    sem_ccv = P.sem("ccv")
    P.op("pool", lambda e: e.collective_compute("AllGather", ALU.bypass, replica_groups=[list(range(NCORE))],
                                                ins=[kT_loc], outs=[kT_all]), kv_store_evs, [(sem_cck, 1)])
    P.op("pool", lambda e: e.collective_compute("AllGather", ALU.bypass, replica_groups=[list(range(NCORE))],
                                                ins=[v_loc], outs=[v_all]), (), [(sem_ccv, 1)])
    ev_cc = (sem_cck, 1)
    ev_ccv = (sem_ccv, 1)

    linear(w_in, 0, 32, [Q0 + i * 256 for i in range(16)], hT, T1024H, epi_store_bf16(qT_d, Q0))

    valb = salloc(F32, 2, NTH)
    cbuf = [salloc(F32, NTH) for _ in range(2)]
    cacc = Ring(2, F32, NT)
    cst = {"i": 0, "cfree": [None, None]}
    P.c("dve", lambda e: e.memset(acc1, 0.0))
    P.c("dve", lambda e: e.memset(acc2, 0.0))

    def epi_val(col, pst, evp):
        o = ((col - CV0) // 128) % 2
        ev = P.c("dve", lambda e: e.tensor_copy(out=valb[:, o, :], in_=pst[:, 0:NTH]), waits=[evp])
        return [ev]

    def epi_cgate(col, pst, evp):
        ch = (col - CG0) // 128
        o = ch % 2
        k = cst["i"] % 2
        cst["i"] += 1
        cb = cbuf[k]
        eva = P.c("act", lambda e: e.activation(out=cb, in_=pst[:, 0:NTH], func=AF.Sigmoid), waits=[evp, cst["cfree"][k]])
        P.c("dve", lambda e: e.tensor_tensor(out=cb, in0=cb, in1=valb[:, o, :], op=ALU.mult), waits=[eva])
        buf, sem, evfree = cacc.next()
        P.c("dve", lambda e: e.tensor_scalar(out=buf, in0=cb[:, 1:1 + NT], scalar1=c_wdw[:, ch, 0:1], scalar2=vec(VB_DW, ch),
                                             op0=ALU.mult, op1=ALU.add), waits=[evfree])
        for j in range(1, 31):
            evd = P.c("dve", lambda e, j=j: e.scalar_tensor_tensor(out=buf, in0=cb[:, 1 + j:1 + j + NT], scalar=c_wdw[:, ch, j:j + 1],
                                                                   in1=buf, op0=ALU.mult, op1=ALU.add))
        cst["cfree"][k] = evd
        P.d("sp", lambda e: e.dma_start(out=conv_d[ch * 128:(ch + 1) * 128, :], in_=buf), sem, waits=[evd])
        P.c("dve", lambda e: e.tensor_tensor(out=acc1, in0=acc1, in1=buf, op=ALU.add))
        evq = P.c("act", lambda e: e.activation(out=sqtmp, in_=buf, func=AF.Square),
                  waits=[evd, (P.prog["dve"], P.prog["dve"].v - 0)])
        P.c("dve", lambda e: e.tensor_tensor(out=acc2, in0=acc2, in1=sqtmp, op=ALU.add), waits=[evq])
        return [eva]

    for i in range(16):
        linear(w_in, 0, 32, [CV0 + i * 256], hT, T1056, epi_val)
        linear(w_in, 0, 32, [CG0 + i * 256], hT, T1056, epi_cgate)

    gst = Ring(2, F32, NT)

    def epi_gate(col, pst, evp):
        buf, sem, evfree = gst.next()
        eva = P.c("act", lambda e: e.activation(out=buf, in_=pst[:, 0:NT], func=AF.Sigmoid), waits=[evp, evfree])
        r = col - G0
        P.d("sp", lambda e: e.dma_start(out=gate_d[r:r + 128, :], in_=buf), sem, waits=[eva])
        return [eva]

    linear(w_in, 0, 32, [G0 + i * 256 for i in range(32)], hT, T1024H, epi_gate)
    P.barrier()
    M.top = MARK1

    cact = salloc(BF16, 32, NT)
    MARK3 = M.top
    mean_b = salloc(F32, NT)
    rstd_c = salloc(F32, NT)
    for t in range(2):
        P.c("pe", lambda e, t=t: e.matmul(bank(0 + t), lhsT=c_ones, rhs=acc1[:, t * 512:(t + 1) * 512], start=True, stop=True))
    for t in range(2):
        evp = P.c("pe", lambda e, t=t: e.matmul(bank(3 + t), lhsT=c_ones, rhs=acc2[:, t * 512:(t + 1) * 512], start=True, stop=True))
    P.c("dve", lambda e: e.tensor_scalar(out=mean_b, in0=ps[:, 0:NT], scalar1=1.0 / D, scalar2=None, op0=ALU.mult), waits=[evp])
    P.c("dve", lambda e: e.tensor_tensor(out=sqtmp, in0=mean_b, in1=mean_b, op=ALU.mult))
    evd = P.c("dve", lambda e: e.scalar_tensor_tensor(out=rstd_c, in0=ps[:, 1536:1536 + NT], scalar=1.0 / D, in1=sqtmp,
                                                      op0=ALU.mult, op1=ALU.subtract))
    eva = P.c("act", lambda e: e.activation(out=rstd_c, in_=rstd_c, func=AF.Sqrt, bias=EPS, scale=1.0), waits=[evd])
    evr = P.c("dve", lambda e: e.reciprocal(out=rstd_c, in_=rstd_c), waits=[eva])
    cin = [salloc(F32, NT) for _ in range(2)]
    cinsem = [P.sem("cin") for _ in range(2)]
    cfree = [None, None]
    for ch in range(32):
        b = ch % 2
        evl = P.d("sp", lambda e, b=b, ch=ch: e.dma_start(out=cin[b], in_=conv_d[ch * 128:(ch + 1) * 128, :]), cinsem[b],
                  waits=[cfree[b]])
        P.c("dve", lambda e, b=b: e.tensor_tensor(out=cin[b], in0=cin[b], in1=mean_b, op=ALU.subtract), waits=[evl, evr])
        evd = P.c("dve", lambda e, b=b: e.tensor_tensor(out=cin[b], in0=cin[b], in1=rstd_c, op=ALU.mult))
        cfree[b] = P.c("act", lambda e, b=b, ch=ch: e.activation(out=cact[:, ch, :], in_=cin[b], func=AF.Silu,
                                                                 bias=vec(VB_CLN, ch), scale=vec(VG_CLN, ch)), waits=[evd])
    P.barrier()
    M.top = MARK3

    def gate_loader(row0, bufs=None):
        gb = bufs if bufs is not None else [salloc(F32, NT) for _ in range(2)]
        gsem = [P.sem("gl") for _ in range(2)]
        st = {"free": [None, None], "ev": {}}

        def issue(ch):
            if ch >= 32:
                return
            b = ch % 2
            st["ev"][ch] = P.d("sp", lambda e: e.dma_start(out=gb[b], in_=gate_d[row0 + ch * 128:row0 + (ch + 1) * 128, :]),
                               gsem[b], waits=[st["free"][b]])
        return gb, st, issue

    gb, gstt, gissue = gate_loader(D)
    sst = Ring(2, F32, NT)
    gissue(0)

    def epi_convo(col, pst, evp):
        ch = col // 128
        b = ch % 2
        gissue(ch + 1)
        buf, sem, evfree = sst.next()
        evd = P.c("dve", lambda e: e.tensor_tensor(out=buf, in0=pst[:, 0:NT], in1=gb[b], op=ALU.mult),
                  waits=[evp, gstt["ev"][ch], evfree])
        gstt["free"][b] = evd
        P.d("sp", lambda e: e.dma_start(out=stash_d[ch * 128:(ch + 1) * 128, :], in_=buf), sem, waits=[evd])
        return [evd]

    linear(w_conv_o, 0, 32, [i * 256 for i in range(16)], cact, T1024, epi_convo)
    P.barrier(engines=("pe", "act", "dve", "sp", "pool"))
    M.top = MARK0

    Kh = [salloc(BF16, 2, SEQ) for _ in range(2)]
    Vh = [salloc(BF16, 64, 257) for _ in range(2)]
    Qh = [salloc(BF16, 2, NT) for _ in range(2)]
    kvsem = [P.sem("kv") for _ in range(2)]
    NBG = 8
    bring = [salloc(F32, NBG, 256) for _ in range(3)]
    bsem = [P.sem("bs") for _ in range(3)]
    tmpb = [salloc(F32, 2, 256) for _ in range(2)]
    Pb = [salloc(BF16, 2, 256) for _ in range(3)]
    osb = [salloc(F32, 256) for _ in range(2)]
    onb = [salloc(BF16, 256) for _ in range(2)]
    junk = salloc(F32, 256)
    rsb = salloc(F32, 8)
    oTst = [salloc(BF16, 2, NT) for _ in range(2)]
    otsem = [P.sem("ot") for _ in range(2)]
    psT = ps[:, 6 * 512:7 * 512].bitcast(BF16)
    for b in range(2):
        P.c("dve", lambda e, b=b: e.memset(Vh[b][:, :, 256:257], 1.0))
    ev_ones = (P.prog["dve"], P.prog["dve"].v)

    kv_free = [None, None]
    kv_ev = {}

    def load_head(h):
        b = h % 2
        w = [ev_cc, ev_ccv, kv_free[b]]
        for m in range(2):
            src = bass.AP(kT_all.tensor, (h * 256 + m * 128) * NT, [[NT, 128], [D * NT, NCORE], [1, NT]])
            P.d("sp", lambda e, src=src, b=b, m=m: e.dma_start(out=Kh[b][:, m, :].rearrange("p (r t) -> p r t", t=NT), in_=src),
                kvsem[b], waits=w)
        src = bass.AP(v_all.tensor, h * 256, [[D, 128], [128 * D, 64], [1, 256]])
        P.d("sp", lambda e, src=src, b=b: e.dma_start(out=Vh[b][:, :, 0:256], in_=src), kvsem[b], waits=w + [ev_ones])
        src = qT_d[h * 256:(h + 1) * 256, :].rearrange("(m p) t -> p m t", p=128)
        P.d("sp", lambda e, src=src, b=b: e.dma_start(out=Qh[b], in_=src), kvsem[b], waits=w)
        kv_ev[h] = (kvsem[b], kvsem[b].v)

    bst = {"i": 0, "free": [None, None, None]}

    def load_bias(h, qb, g):
        k = bst["i"] % 3
        bst["i"] += 1
        off = h * 128 * ZROW + (GOFF - 63 * 128) + qb * 256 + g * NBG * 128
        src = bass.AP(Z_t, off, [[GW, 128], [128, NBG], [1, 256]])
        ev = P.d("sp", lambda e, src=src, k=k: e.dma_start(out=bring[k], in_=src), bsem[k], waits=[ev_z, bst["free"][k]])
        return k, ev

    sfree = [None, None]
    tfree = [None, None]
    pfree = [None, None, None]
    accfree = [None]
    pend_tr = []
    trfree = [None]
    otfree = [None, None]
    ot_store_evs = []

    def flush_transposes():
        while pend_tr:
            fn = pend_tr.pop(0)
            fn()

    load_head(0)
    blk = 0
    for h in range(NH):
        hb = h % 2
        if h + 1 < NH:
            load_head(h + 1)
        for qb in range(4):
            biasg = {}
            biasg[0] = load_bias(h, qb, 0)
            av_q = []
            for i in range(64):
                kc = 63 - i
                g, ii = divmod(i, NBG)
                if ii == 0 and g + 1 < 64 // NBG:
                    biasg[g + 1] = load_bias(h, qb, g + 1)
                bk, bev = biasg[g]
                sb_ = i % 2
                psS = bank(4 + sb_)
                for m in range(2):
                    evs = P.c("pe", lambda e, psS=psS, m=m, kc=kc, hb=hb, qb=qb: e.matmul(
                        psS[:, m * 256:(m + 1) * 256], lhsT=Kh[hb][:, m, kc * 128:(kc + 1) * 128],
                        rhs=Qh[hb][:, m, qb * 256:(qb + 1) * 256], start=True, stop=True),
                        waits=[kv_ev[h], sfree[sb_]] if m == 0 else (), signal=(m == 1))
                tb_ = i % 2
                for m in range(2):
                    evd = P.c("dve", lambda e, psS=psS, m=m, tb_=tb_, bk=bk, ii=ii: e.scalar_tensor_tensor(
                        out=tmpb[tb_][:, m, :], in0=psS[:, m * 256:(m + 1) * 256], scalar=SCALE, in1=bring[bk][:, ii, :],
                        op0=ALU.mult, op1=ALU.add), waits=[evs, bev, tfree[tb_]] if m == 0 else ())
                sfree[sb_] = evd
                if ii == NBG - 1:
                    bst["free"][bk] = evd
                pk = i % 3
                eva = P.c("act", lambda e, pk=pk, tb_=tb_: e.activation(out=Pb[pk], in_=tmpb[tb_], func=AF.Exp),
                          waits=[evd, pfree[pk]])
                tfree[tb_] = eva

                def av(i=i, kc=kc, pk=pk, eva=eva, hb=hb):
                    n = 0
                    for m in range(2):
                        for js in range(2):
                            n += 1
                            w = [eva]
                            if i == 0:
                                w.append(accfree[0])
                            evp = P.c("pe", lambda e, m=m, js=js: e.matmul(
                                bank(m * 2 + js, 257), lhsT=Pb[pk][:, m, js * 128:(js + 1) * 128], rhs=Vh[hb][:, kc, :],
                                start=(i == 0), stop=(i == 63)), waits=w if n == 1 else (), signal=(n == 4))
                    pfree[pk] = evp
                    return evp
                av_q.append(av)
                if len(av_q) > 1:
                    ev_av = av_q.pop(0)()
                if i == 8:
                    flush_transposes()
            ev_av = av_q.pop(0)()
            if h == NH - 1 or qb == 3:
                pass
            ob = blk % 2
            blk += 1
            for js in range(2):
                k2 = (blk + js) % 2
                a0 = bank(0 + js, 257)
                a1 = bank(2 + js, 257)
                e0 = P.c("dve", lambda e, a0=a0: e.reciprocal(out=rsb[:, 0:1], in_=a0[:, 256:257]), waits=[ev_av])
                e1 = P.c("dve", lambda e, a1=a1: e.reciprocal(out=rsb[:, 1:2], in_=a1[:, 256:257]))
                e1 = P.c("dve", lambda e: e.tensor_tensor(out=rsb[:, 1:2], in0=rsb[:, 1:2], in1=neglam, op=ALU.mult), waits=[e1])
                e2 = P.c("dve", lambda e, a0=a0, k2=k2: e.tensor_scalar(out=osb[k2], in0=a0[:, 0:256], scalar1=rsb[:, 0:1],
                                                                        scalar2=None, op0=ALU.mult), waits=[e1])
                evd = P.c("dve", lambda e, a1=a1, k2=k2: e.scalar_tensor_tensor(out=osb[k2], in0=a1[:, 0:256], scalar=rsb[:, 1:2],
                                                                                in1=osb[k2], op0=ALU.mult, op1=ALU.add), waits=[e2])
                if js == 1:
                    accfree[0] = evd
                eva = P.c("act", lambda e, k2=k2, js=js: e.activation(out=junk, in_=osb[k2], func=AF.Square,
                                                                      accum_out=rsb[:, 2 + js:3 + js]), waits=[evd])
                eva = P.c("act", lambda e, js=js: e.activation(out=rsb[:, 4 + js:5 + js], in_=rsb[:, 2 + js:3 + js], func=AF.Sqrt,
                                                               bias=EPS, scale=1.0 / 256), waits=[eva])
                e3 = P.c("dve", lambda e, js=js: e.reciprocal(out=rsb[:, 6 + js:7 + js], in_=rsb[:, 4 + js:5 + js]), waits=[eva])
                evn = P.c("dve", lambda e, k2=k2, js=js: e.tensor_scalar(out=onb[k2], in0=osb[k2], scalar1=rsb[:, 6 + js:7 + js],
                                                                         scalar2=None, op0=ALU.mult), waits=[e3])

                def tr(evn=evn, k2=k2, js=js, h=h, qb=qb):
                    hb2 = h % 2
                    for half in range(2):
                        col = (js * 2 + half) * 128
                        evp = P.c("pe", lambda e, half=half, col=col: e.transpose(
                            out=psT[:, col:col + 128], in_=onb[k2][:, half * 128:(half + 1) * 128], identity=c_ident),
                            waits=[evn, trfree[0]] if half == 0 else ())
                        q0 = qb * 256 + js * 128
                        evc = P.c("dve", lambda e, half=half, col=col, q0=q0: e.tensor_scalar(
                            out=oTst[hb2][:, half, q0:q0 + 128], in0=psT[:, col:col + 128], scalar1=c_sub[:, half:half + 1],
                            scalar2=1.0 - LAMBDA_INIT, op0=ALU.mult, op1=ALU.mult), waits=[evp, otfree[hb2]])
                    trfree[0] = evc
                    if qb == 3 and js == 1:
                        dst = oT_d[h * 256:(h + 1) * 256, :].rearrange("(a p) t -> p a t", p=128)
                        ot_store_evs.append(P.d("sp", lambda e: e.dma_start(out=dst, in_=oTst[hb2]), otsem[hb2], waits=[evc]))
                        otfree[hb2] = (otsem[hb2], otsem[hb2].v)
                pend_tr.append(tr)
            if h == NH - 1 and qb == 3:
                flush_transposes()
        kv_free[hb] = ev_av
    P.barrier(engines=("pe", "act", "dve", "sp", "pool"))
    M.top = MARK1

    merged = salloc(BF16, 32, NT)
    MARK4 = M.top
    oact = salloc(BF16, 32, NT)
    sem_o = P.sem("oact")
    ev_o = P.d("sp", lambda e: e.dma_start(out=oact, in_=oT_d.rearrange("(kc p) t -> p kc t", p=128)), sem_o)
    gb2, gstt2, gissue2 = gate_loader(0, [acc1, acc2])
    stb = [sqtmp, rstd_b]
    stsem = [P.sem("stl") for _ in range(2)]
    stst = {"free": [None, None], "ev": {}}

    def stash_issue(ch):
        if ch >= 32:
            return
        b = ch % 2
        stst["ev"][ch] = P.d("sp", lambda e: e.dma_start(out=stb[b], in_=stash_d[ch * 128:(ch + 1) * 128, :]), stsem[b],
                             waits=[stst["free"][b]])
    gissue2(0)
    stash_issue(0)

    def epi_attno(col, pst, evp):
        ch = col // 128
        b = ch % 2
        gissue2(ch + 1)
        stash_issue(ch + 1)
        evd = P.c("dve", lambda e: e.tensor_tensor(out=gb2[b], in0=pst[:, 0:NT], in1=gb2[b], op=ALU.mult),
                  waits=[evp, gstt2["ev"][ch]])
        ev2 = P.c("dve", lambda e: e.tensor_tensor(out=merged[:, ch, :], in0=gb2[b], in1=stb[b], op=ALU.add),
                  waits=[stst["ev"][ch]])
        gstt2["free"][b] = ev2
        stst["free"][b] = ev2
        return [evd]

    P.op("pe", None, [ev_o], ())
    linear(w_attn_o, 0, 32, [i * 256 for i in range(16)], oact, T1024, epi_attno)
    P.barrier()
    M.top = MARK4

    mst = Ring(2, F32, NT)
    P.c("dve", lambda e: e.memset(acc2, 0.0))

    def epi_store_stats(dst):
        def epi(col, pst, evp):
            ch = col // 128
            buf, sem, evfree = mst.next()
            eva = P.c("act", lambda e: e.activation(out=buf, in_=pst[:, 0:NT], func=AF.Copy), waits=[evp, evfree])
            P.d("sp", lambda e: e.dma_start(out=dst[ch * 128:(ch + 1) * 128, :], in_=buf), sem, waits=[eva])
            evq = P.c("act", lambda e: e.activation(out=sqtmp, in_=buf, func=AF.Square),
                      waits=[(P.prog["dve"], P.prog["dve"].v)])
            P.c("dve", lambda e: e.tensor_tensor(out=acc2, in0=acc2, in1=sqtmp, op=ALU.add), waits=[evq])
            return [eva]
        return epi

    linear(w_out, 0, 32, [i * 256 for i in range(16)], merged, T1024, epi_store_stats(mix_d))
    P.barrier()
    M.top = MARK1
    h2T = salloc(BF16, 32, NT)
    MARK5 = M.top

    def resid_pass(src_a, src_b, b_is_x, gidx, rstd, dst, acc=None, bf_out=None):
        ia = [salloc(F32, NT) for _ in range(2)]
        ib = [salloc(F32, NT) for _ in range(2)]
        sa = [P.sem("ia") for _ in range(2)]
        sb_ = [P.sem("ib") for _ in range(2)]
        so = [P.sem("io") for _ in range(2)]
        fa = [None, None]
        evs = []
        if acc is not None:
            P.c("dve", lambda e: e.memset(acc, 0.0))
        for ch in range(32):
            b = ch % 2
            rows = slice(ch * 128, (ch + 1) * 128)
            e1 = P.d("sp", lambda e, b=b, rows=rows: e.dma_start(out=ia[b], in_=src_a[rows, :]), sa[b], waits=[fa[b]])
            srcb = src_b[rows, HAL:HAL + NT] if b_is_x else src_b[rows, :]
            e2 = P.d("sp", lambda e, b=b, srcb=srcb: e.dma_start(out=ib[b], in_=srcb), sb_[b],
                     waits=[fa[b], (so[b], so[b].v)])
            P.c("dve", lambda e, b=b, ch=ch: e.scalar_tensor_tensor(out=ia[b], in0=ia[b], scalar=vec(gidx, ch), in1=rstd,
                                                                    op0=ALU.mult, op1=ALU.mult), waits=[e1])
            evd = P.c("dve", lambda e, b=b: e.tensor_tensor(out=ib[b], in0=ib[b], in1=ia[b], op=ALU.add), waits=[e2])
            evs.append(P.d("sp", lambda e, b=b, rows=rows: e.dma_start(out=dst[rows, :], in_=ib[b]), so[b], waits=[evd]))
            last = evd
            if acc is not None:
                evq = P.c("act", lambda e, b=b: e.activation(out=sqtmp, in_=ib[b], func=AF.Square),
                          waits=[evd, (P.prog["dve"], P.prog["dve"].v)])
                last = P.c("dve", lambda e: e.tensor_tensor(out=acc, in0=acc, in1=sqtmp, op=ALU.add), waits=[evq])
            if bf_out is not None:
                last2 = P.c("act", lambda e, b=b, ch=ch: e.activation(out=bf_out[:, ch, :], in_=ib[b], func=AF.Copy), waits=[evd])
                fa[b] = last2
                P.op("dve", None, [last2], ())
            else:
                fa[b] = last
        return evs

    stats_finish(acc2, 1.0 / D, rstd_b)
    resid_pass(mix_d, xT, True, VG_MPOST, rstd_b, x1_d, acc=acc1)
    P.barrier()
    M.top = MARK5
    stats_finish(acc1, 1.0 / D, rstd_b)
    xl = [salloc(F32, NT) for _ in range(2)]
    xls = [P.sem("xl2") for _ in range(2)]
    xlf = [None, None]
    for ch in range(32):
        b = ch % 2
        evl = P.d("sp", lambda e, b=b, ch=ch: e.dma_start(out=xl[b], in_=x1_d[ch * 128:(ch + 1) * 128, :]), xls[b], waits=[xlf[b]])
        xlf[b] = P.c("dve", lambda e, b=b, ch=ch: e.scalar_tensor_tensor(out=h2T[:, ch, :], in0=xl[b], scalar=vec(VG_FPRE, ch),
                                                                         in1=rstd_b, op0=ALU.mult, op1=ALU.mult), waits=[evl])
    P.barrier()
    M.top = MARK5

    uT = salloc(BF16, 32, NT)
    rtmp = [acc1, rstd_b]
    fin = [salloc(F32, NT) for _ in range(2)]
    finsem = [P.sem("fin") for _ in range(2)]
    fstsem = [P.sem("fst") for _ in range(2)]
    ffn_store = {}
    rst = {"i": 0, "free": [None, None]}
    P.c("dve", lambda e: e.memset(acc2, 0.0))
    for qq in range(4):
        def epi_up(col, pst, evp, qq=qq):
            ch = (col - qq * D) // 128
            k = rst["i"] % 2
            rst["i"] += 1
            evd = P.c("dve", lambda e: e.tensor_scalar(out=rtmp[k], in0=pst[:, 0:NT], scalar1=0.0, scalar2=None, op0=ALU.max),
                      waits=[evp, rst["free"][k]])
            rst["free"][k] = P.c("act", lambda e: e.activation(out=uT[:, ch, :], in_=rtmp[k], func=AF.Square), waits=[evd])
            return [evd]

        linear(w_up, 0, 32, [qq * D + i * 256 for i in range(16)], h2T, T1024, epi_up)
        P.op("pe", None, [(P.prog["act"], P.prog["act"].v)], ())
        fst8 = {"ev": {}}

        def fin_issue(ch, qq=qq):
            if ch >= 32 or qq == 0:
                return
            b = ch % 2
            fst8["ev"][ch] = P.d("sp", lambda e: e.dma_start(out=fin[b], in_=ffn_d[ch * 128:(ch + 1) * 128, :]), finsem[b],
                                 waits=[(fstsem[b], fstsem[b].v), ffn_store[ch]])
        fin_issue(0)

        def epi_down(col, pst, evp, qq=qq):
            ch = col // 128
            b = ch % 2
            fin_issue(ch + 1)
            buf = fin[b]
            if qq == 0:
                ev = P.c("act", lambda e: e.activation(out=buf, in_=pst[:, 0:NT], func=AF.Copy),
                         waits=[evp, (fstsem[b], fstsem[b].v)])
            else:
                ev = P.c("dve", lambda e: e.tensor_tensor(out=buf, in0=pst[:, 0:NT], in1=buf, op=ALU.add),
                         waits=[evp, fst8["ev"][ch]])
            ffn_store[ch] = P.d("sp", lambda e: e.dma_start(out=ffn_d[ch * 128:(ch + 1) * 128, :], in_=buf), fstsem[b], waits=[ev])
            if qq == 3:
                evq = P.c("act", lambda e: e.activation(out=sqtmp, in_=buf, func=AF.Square),
                          waits=[ev, (P.prog["dve"], P.prog["dve"].v)])
                P.c("dve", lambda e: e.tensor_tensor(out=acc2, in0=acc2, in1=sqtmp, op=ALU.add), waits=[evq])
            return [ev]

        linear(w_down, qq * D, 32, [i * 256 for i in range(16)], uT, T1024, epi_down)
    P.barrier()
    M.top = MARK1

    x2T = salloc(BF16, 32, NT)
    MARK6 = M.top
    stats_finish(acc2, 1.0 / D, rstd_b)
    resid_pass(ffn_d, x1_d, False, VG_FPOST, rstd_b, x2_d, acc=None, bf_out=x2T)
    P.barrier()
    M.top = MARK6

    pbf = salloc(BF16, 2, NT)
    wpp = salloc(BF16, 2, D)
    sem_p = P.sem("ple")
    P.d("pool", lambda e: e.dma_start(out=pbf, in_=pT.rearrange("(kc p) t -> p kc t", p=128)), sem_p)
    P.d("pool", lambda e: e.dma_start(out=wpp, in_=w_ple_proj.rearrange("(kc p) n -> p kc n", p=128)), sem_p)
    ev_p = (sem_p, sem_p.v)
    P.dma_out = [x for x in P.dma_out if x[0] is not sem_p]
    sgt = [salloc(F32, NT) for _ in range(2)]
    tst = Ring(2, F32, NT)
    P.c("dve", lambda e: e.memset(acc2, 0.0))
    pst8 = {"efree": None, "i": 0, "sfree": [None, None]}

    def epi_ple(col, pst, evp):
        ch = col // 128
        k = pst8["i"] % 2
        pst8["i"] += 1
        n = 0
        for kc in range(2):
            for t in range(2):
                n += 1
                evE = P.c("pe", lambda e, kc=kc, t=t: e.matmul(bank(6 + t), lhsT=wpp[:, kc, ch * 128:(ch + 1) * 128],
                                                                rhs=pbf[:, kc, t * 512:(t + 1) * 512], start=(kc == 0), stop=(kc == 1)),
                          waits=[ev_p, pst8["efree"]] if n == 1 else (), signal=(n == 4))
        eva = P.c("act", lambda e: e.activation(out=sgt[k], in_=pst[:, 0:NT], func=AF.Sigmoid), waits=[evp, pst8["sfree"][k]])
        buf, sem, evfree = tst.next()
        evd = P.c("dve", lambda e: e.tensor_tensor(out=buf, in0=ps[:, 6 * 512:8 * 512], in1=sgt[k], op=ALU.mult),
                  waits=[evE, eva, evfree])
        pst8["efree"] = evd
        pst8["sfree"][k] = evd
        P.d("sp", lambda e: e.dma_start(out=t_d[ch * 128:(ch + 1) * 128, :], in_=buf), sem, waits=[evd])
        evq = P.c("act", lambda e: e.activation(out=sqtmp, in_=buf, func=AF.Square), waits=[evd, (P.prog["dve"], P.prog["dve"].v)])
        P.c("dve", lambda e: e.tensor_tensor(out=acc2, in0=acc2, in1=sqtmp, op=ALU.add), waits=[evq])
        return [eva]

    linear(w_ple_gate, 0, 32, [i * 256 for i in range(16)], x2T, T1024, epi_ple)
    P.barrier()
    stats_finish(acc2, 1.0 / D, rstd_b)
    out_evs = resid_pass(t_d, x2_d, False, VG_PPOST, rstd_b, outT)
    P.barrier(engines=("pe", "act", "dve", "sp", "pool"))
    for nm, src, eng in (("stash0", stash_d[0:128, :], "sp"), ("stash31", stash_d[3968:4096, :], "sp"),
                         ("mix0", mix_d[0:128, :], "sp"), ("mix31", mix_d[3968:4096, :], "sp"),
                         ("x1_0", x1_d[0:128, :], "sp"), ("ffn0", ffn_d[0:128, :], "sp"), ("ffn31", ffn_d[3968:4096, :], "sp"),
                         ("x2_0", x2_d[0:128, :], "sp"), ("t0", t_d[0:128, :], "sp"), ("t31", t_d[3968:4096, :], "sp")):
        dump(nm, src, NT, eng=eng)
    DBG_NAMES[:] = dbg_names

    with nc.Block() as block:
        P.emit(block)
    return nc


def _rel_bucket_np(rel):
    nb = 16
    me = 8
    ret = (rel > 0).astype(np.int32) * nb
    n = np.abs(rel)
    nf = np.maximum(n, 1).astype(np.float32)
    large = me + (np.log(nf / np.float32(me)) / np.float32(math.log(128 / me)) * np.float32(nb - me)).astype(np.int32)
    large = np.minimum(large, nb - 1)
    return ret + np.where(n < me, n, large)


_NC_CACHE = {}


def kernel(x, p, positions, rel_table, mix_pre_g, w_in, lambda_q1, lambda_k1, lambda_q2, lambda_k2, subln_g,
           w_attn_o, w_dw, b_dw, conv_ln_g, conv_ln_b, w_conv_o, w_out, mix_post_g, ffn_pre_g, w_up, w_down,
           ffn_post_g, w_ple_gate, w_ple_proj, ple_post_g):
    f32 = np.float32
    x = np.asarray(x, f32)[0]
    pp = np.asarray(p, f32)[0, 0]
    xTfull = np.zeros((D, SEQ + 2 * HAL), f32)
    xTfull[:, HAL:HAL + SEQ] = x.T

    def v128(v):
        return np.asarray(v, f32).reshape(32, 128).T

    vecs = np.stack([v128(mix_pre_g[0]), v128(b_dw[0]), v128(conv_ln_g[0]), v128(conv_ln_b[0]), v128(mix_post_g[0]),
                     v128(ffn_pre_g[0]), v128(ffn_post_g[0]), v128(ple_post_g[0])], axis=1)
    wdw = np.asarray(w_dw, f32)[0, :, 0, :]
    wdw = wdw.T.reshape(32, 128, 31).transpose(1, 0, 2)
    lamv = np.stack([np.asarray(lambda_q1, f32)[0], np.asarray(lambda_q2, f32)[0],
                     np.asarray(lambda_k1, f32)[0], np.asarray(lambda_k2, f32)[0]], axis=1)
    subl = np.asarray(subln_g, f32)[0].reshape(2, 128).T
    ident = np.eye(128, dtype=f32).astype(ml_dtypes.bfloat16)
    common = {
        "vecs": np.ascontiguousarray(vecs.reshape(128, 256)),
        "wdw": np.ascontiguousarray(wdw.reshape(128, 32 * 31)),
        "lamv": np.ascontiguousarray(lamv),
        "subln": np.ascontiguousarray(subl),
        "relt": np.ascontiguousarray(np.asarray(rel_table, f32)),
        "ident": ident,
        "w_in": np.asarray(w_in, f32)[0], "w_attn_o": np.asarray(w_attn_o, f32)[0],
        "w_conv_o": np.asarray(w_conv_o, f32)[0], "w_out": np.asarray(w_out, f32)[0],
        "w_up": np.asarray(w_up, f32)[0], "w_down": np.asarray(w_down, f32)[0],
        "w_ple_gate": np.asarray(w_ple_gate, f32)[0], "w_ple_proj": np.asarray(w_ple_proj, f32)[0],
    }
    in_maps = []
    ii = np.arange(GW, dtype=np.int64)
    for c in range(NCORE):
        rel = (GOFF - ii - c * NT).astype(np.int32)
        bk = _rel_bucket_np(rel)
        oh = (bk[None, :] == np.arange(32)[:, None]).astype(f32)
        m = dict(common)
        m["xT"] = np.ascontiguousarray(xTfull[:, c * NT:c * NT + NTH])
        m["pT"] = np.ascontiguousarray(pp[c * NT:(c + 1) * NT].T)
        m["oh"] = oh
        in_maps.append(m)
    if "nc" not in _NC_CACHE:
        _NC_CACHE["nc"] = build()
    res = run_bass_kernel_spmd(_NC_CACHE["nc"], in_maps, core_ids=list(range(NCORE)))
    if DEBUG:
        _NC_CACHE["res"] = res
    out = np.concatenate([np.asarray(r["outT"], f32).T for r in res.results], axis=0)
    return out[None].astype(f32)
```
